# Optimizing a Trainium2 kernel written in Bass

```python
import math
import jax, jax.numpy as jnp
from jax import lax
import numpy as np

D_MODEL = 1024
BATCH = 1
SEQ = 16384
DEPTH = 2

N_A = DEPTH // 2
N_B = DEPTH - N_A
CONV_WIDTH = 31
D_FF = ((8 * D_MODEL + 3 * 256 - 1) // (3 * 256)) * 256
HEAD_DIM = 64
N_Q_HEADS = D_MODEL // HEAD_DIM
N_KV_HEADS = 2
GROUP = N_Q_HEADS // N_KV_HEADS
WINDOW = 128
BLOCK = 128
ALIBI_MAX = 8.0
ALPHA = (2.0 * DEPTH) ** 0.25
BETA = (8.0 * DEPTH) ** -0.25
LN_EPS = 1e-5
NEG_INF = -1e30

kernel_name = "yoco_conformer_swa_sink_alibi_deepnorm"


def layer_norm(x, g, b):
    xf = x.astype(jnp.float32)
    mu = jnp.mean(xf, axis=-1, keepdims=True)
    var = jnp.mean(jnp.square(xf - mu), axis=-1, keepdims=True)
    y = (xf - mu) * lax.rsqrt(var + LN_EPS)
    return (y * g.astype(jnp.float32) + b.astype(jnp.float32)).astype(x.dtype)


def conformer_conv(x, w_pw1, b_pw1, w_dw, b_dw, ln_g, ln_b, w_pw2, b_pw2):
    h = x @ w_pw1 + b_pw1
    h = h[..., :D_MODEL] * jax.nn.sigmoid(h[..., D_MODEL:])
    h = lax.conv_general_dilated(
        h, w_dw[:, None, :].astype(h.dtype), window_strides=(1,),
        padding=[(CONV_WIDTH - 1, 0)],
        dimension_numbers=("NWC", "WIO", "NWC"),
        feature_group_count=D_MODEL) + b_dw
    h = jax.nn.silu(layer_norm(h, ln_g, ln_b))
    return h @ w_pw2 + b_pw2


def swiglu(x, w_gate, w_up, w_down):
    return (jax.nn.silu(x @ w_gate) * (x @ w_up)) @ w_down


def banded_blocks(t):
    b, s = t.shape[0], t.shape[1]
    nb = s // BLOCK
    pad = jnp.zeros((b, BLOCK) + t.shape[2:], t.dtype)
    prev = jnp.concatenate([pad, t[:, :s - BLOCK]], axis=1).reshape(b, nb, BLOCK, *t.shape[2:])
    cur = t.reshape(b, nb, BLOCK, *t.shape[2:])
    return jnp.concatenate([prev, cur], axis=2)


def shared_kv(h, w_k, b_k, w_v, b_v):
    b, s, _ = h.shape
    k = (h @ w_k + b_k).reshape(b, s, N_KV_HEADS, HEAD_DIM)
    v = (h @ w_v + b_v).reshape(b, s, N_KV_HEADS, HEAD_DIM)
    return banded_blocks(k), banded_blocks(v)


def window_attention(x, k_blk, v_blk, w_q, b_q, sinks, w_o, b_o):
    b, s, _ = x.shape
    nb = s // BLOCK
    q = (x @ w_q + b_q).reshape(b, nb, BLOCK, N_KV_HEADS, GROUP, HEAD_DIM)
    scores = jnp.einsum("bnikgd,bnjkd->bnkgij", q, k_blk).astype(jnp.float32)
    scores = scores * (1.0 / math.sqrt(HEAD_DIM))
    qi = jnp.arange(BLOCK)[:, None]
    kj = jnp.arange(2 * BLOCK)[None, :]
    delta = qi + BLOCK - kj
    key_pos = jnp.arange(nb)[:, None, None] * BLOCK - BLOCK + kj[None]
    valid = (delta >= 0) & (delta < WINDOW) & (key_pos >= 0)
    slopes = jnp.exp2(-ALIBI_MAX * jnp.arange(1, N_Q_HEADS + 1, dtype=jnp.float32) / N_Q_HEADS)
    slopes = slopes.reshape(N_KV_HEADS, GROUP)
    scores = scores - slopes[None, None, :, :, None, None] * delta.astype(jnp.float32)[None, None, None, None]
    scores = jnp.where(valid[None, :, None, None], scores, NEG_INF)
    sink = jnp.broadcast_to(
        sinks.astype(jnp.float32).reshape(N_KV_HEADS, GROUP)[None, None, :, :, None, None],
        scores.shape[:-1] + (1,))
    probs = jax.nn.softmax(jnp.concatenate([scores, sink], axis=-1), axis=-1)[..., :-1]
    o = jnp.einsum("bnkgij,bnjkd->bnikgd", probs.astype(v_blk.dtype), v_blk)
    o = o.reshape(b, s, N_Q_HEADS * HEAD_DIM)
    return o @ w_o + b_o


def setup_inputs(seed: int = 0) -> dict:
    key = jax.random.key(seed)
    ks = jax.random.split(key, 40)
    f32 = jnp.float32
    D, F, HD = D_MODEL, D_FF, N_Q_HEADS * HEAD_DIM
    KVD = N_KV_HEADS * HEAD_DIM

    def nrm(k, shape, scale):
        return jax.random.normal(k, shape, f32) * scale

    def gain(k, shape):
        return 1.0 + 0.05 * jax.random.normal(k, shape, f32)

    return {
        "x": nrm(ks[0], (BATCH, SEQ, D), 1.0),
        "conv_w_pw1": nrm(ks[1], (N_A, D, 2 * D), D ** -0.5),
        "conv_b_pw1": nrm(ks[2], (N_A, 2 * D), 0.02),
        "conv_w_dw": nrm(ks[3], (N_A, CONV_WIDTH, D), CONV_WIDTH ** -0.5),
        "conv_b_dw": nrm(ks[4], (N_A, D), 0.02),
        "conv_ln_g": gain(ks[5], (N_A, D)),
        "conv_ln_b": nrm(ks[6], (N_A, D), 0.02),
        "conv_w_pw2": nrm(ks[7], (N_A, D, D), BETA * D ** -0.5),
        "conv_b_pw2": nrm(ks[8], (N_A, D), 0.02),
        "kv_w_k": nrm(ks[9], (D, KVD), D ** -0.5),
        "kv_b_k": nrm(ks[10], (KVD,), 0.02),
        "kv_w_v": nrm(ks[11], (D, KVD), BETA * D ** -0.5),
        "kv_b_v": nrm(ks[12], (KVD,), 0.02),
        "attn_w_q": nrm(ks[13], (N_B, D, HD), D ** -0.5),
        "attn_b_q": nrm(ks[14], (N_B, HD), 0.02),
        "attn_sinks": nrm(ks[15], (N_B, N_Q_HEADS), 1.0),
        "attn_w_o": nrm(ks[16], (N_B, HD, D), BETA * HD ** -0.5),
        "attn_b_o": nrm(ks[17], (N_B, D), 0.02),
        "ffn_w_gate": nrm(ks[18], (DEPTH, D, F), D ** -0.5),
        "ffn_w_up": nrm(ks[19], (DEPTH, D, F), D ** -0.5),
        "ffn_w_down": nrm(ks[20], (DEPTH, F, D), BETA * F ** -0.5),
        "ln_mix_g": gain(ks[21], (DEPTH, D)),
        "ln_mix_b": nrm(ks[22], (DEPTH, D), 0.02),
        "ln_ffn_g": gain(ks[23], (DEPTH, D)),
        "ln_ffn_b": nrm(ks[24], (DEPTH, D), 0.02),
    }


def reference(x, conv_w_pw1, conv_b_pw1, conv_w_dw, conv_b_dw, conv_ln_g, conv_ln_b,
              conv_w_pw2, conv_b_pw2, kv_w_k, kv_b_k, kv_w_v, kv_b_v,
              attn_w_q, attn_b_q, attn_sinks, attn_w_o, attn_b_o,
              ffn_w_gate, ffn_w_up, ffn_w_down,
              ln_mix_g, ln_mix_b, ln_ffn_g, ln_ffn_b):
    k_blk, v_blk = None, None
    for layer in range(DEPTH):
        if layer < N_A:
            a = layer
            m = conformer_conv(x, conv_w_pw1[a], conv_b_pw1[a], conv_w_dw[a], conv_b_dw[a],
                               conv_ln_g[a], conv_ln_b[a], conv_w_pw2[a], conv_b_pw2[a])
        else:
            l = layer - N_A
            m = window_attention(x, k_blk, v_blk, attn_w_q[l], attn_b_q[l], attn_sinks[l],
                                 attn_w_o[l], attn_b_o[l])
        x = layer_norm(ALPHA * x + m, ln_mix_g[layer], ln_mix_b[layer])
        f = swiglu(x, ffn_w_gate[layer], ffn_w_up[layer], ffn_w_down[layer])
        x = layer_norm(ALPHA * x + f, ln_ffn_g[layer], ln_ffn_b[layer])
        if layer == N_A - 1:
            k_blk, v_blk = shared_kv(x, kv_w_k, kv_b_k, kv_w_v, kv_b_v)
    return x
```

```python
import contextlib
import numpy as np
import concourse.bass as bass
import concourse.mybir as mybir
from concourse.bass_utils import run_bass_kernel_spmd

F32 = mybir.dt.float32
BF16 = mybir.dt.bfloat16
AF = mybir.ActivationFunctionType
ALU = mybir.AluOpType

D = 1024
NCH = 8
FF = 2816
NFC = 22
SEQ = 16384
NCORES = 8
TPC = 2048
HALO = 128
CW = 31
CH = 30
T0 = 1152
T1 = 1024
XC = CH + T0
XTOT = CH + HALO + TPC
ALPHA = float(2.0 ** 0.5)
EPS = 1e-5
NH = 16
NSLOT = 8
DEFER_LN = False
SLOTC = 2816


class Tile:
    __slots__ = ("w", "r")

    def __init__(self):
        self.w = None
        self.r = []


class Op:
    __slots__ = ("eng", "fn", "deps", "idx", "signal", "rank", "dma", "grp", "tag")


class Prog:
    ENGS = ("pe", "dve", "act", "pool", "sp")

    def __init__(self, nc):
        self.nc = nc
        self.ops = {e: [] for e in self.ENGS}
        self.dma_sems = {}
        self.pe_grp_last = []
        self.phase = ""

    def op(self, eng, fn, reads=(), writes=(), dma_sem=None, pe_acc=False, extra=None):
        deps = set()
        for t in reads:
            if t.w is not None:
                deps.add(t.w)
        for t in writes:
            if t.w is not None:
                deps.add(t.w)
            deps.update(t.r)
        if eng == "pe":
            deps = {d for d in deps if not (d[0] == "eng" and d[1] == "pe")}
        if extra:
            deps |= extra
        lst = self.ops[eng]
        o = Op()
        o.eng, o.fn, o.deps, o.idx, o.signal, o.rank, o.dma = eng, fn, deps, len(lst), False, 0, None
        if dma_sem is not None:
            self.dma_sems[dma_sem] = self.dma_sems.get(dma_sem, 0) + 16
            o.dma = (dma_sem, self.dma_sems[dma_sem])
            tok = ("dma", o.dma[0], o.dma[1])
        else:
            tok = ("eng", eng, o.idx)
        o.grp = None
        o.tag = self.phase
        if eng == "pe":
            if pe_acc and self.pe_grp_last:
                self.pe_grp_last[-1] = o.idx
            else:
                self.pe_grp_last.append(o.idx)
            o.grp = len(self.pe_grp_last) - 1
        lst.append(o)
        for t in writes:
            t.w = tok
            t.r = []
        for t in reads:
            if t.w is not tok:
                t.r.append(tok)
        return o

    def emit(self, final_wait_eng="sp"):
        nc = self.nc
        pe_ops = self.ops["pe"]
        for e in self.ENGS:
            for o in self.ops[e]:
                nd = set()
                for d in o.deps:
                    if d[0] == "eng" and d[1] == "pe":
                        d = ("eng", "pe", self.pe_grp_last[pe_ops[d[2]].grp])
                    nd.add(d)
                o.deps = nd
                for d in o.deps:
                    if d[0] == "eng":
                        self.ops[d[1]][d[2]].signal = True
        for e in self.ENGS:
            r = 0
            for o in self.ops[e]:
                if o.signal and o.dma is None:
                    r += 1
                o.rank = r
        with contextlib.ExitStack() as st:
            esem = {e: st.enter_context(nc.semaphore("s_" + e)) for e in self.ENGS}
            dsem = {n: st.enter_context(nc.semaphore("d_" + n)) for n in self.dma_sems}
            block = st.enter_context(nc.Block())
            hw = {"pe": block.tensor, "dve": block.vector, "act": block.scalar,
                  "pool": block.gpsimd, "sp": block.sync}

            def make(e):
                def body(eng):
                    waited = {}
                    for o in self.ops[e]:
                        need = {}
                        for d in o.deps:
                            if d[0] == "eng":
                                key = ("e", d[1])
                                v = self.ops[d[1]][d[2]].rank
                            else:
                                key = ("d", d[1])
                                v = d[2]
                            if v > need.get(key, 0):
                                need[key] = v
                        for key, v in need.items():
                            if waited.get(key, 0) >= v:
                                continue
                            waited[key] = v
                            eng.wait_ge(esem[key[1]] if key[0] == "e" else dsem[key[1]], v)
                        ins = o.fn(eng)
                        if o.dma is not None:
                            ins.then_inc(dsem[o.dma[0]], 16)
                        elif o.signal:
                            ins.then_inc(esem[e], 1)
                    if e == final_wait_eng:
                        for n, v in self.dma_sems.items():
                            if waited.get(("d", n), 0) < v:
                                eng.wait_ge(dsem[n], v)
                return body

            for e in self.ENGS:
                hw[e](make(e))


PCOLS = {}
_off = 0
for _n, _w in [("b_pw1", 16), ("w_dw", 8 * CW), ("b_dw", 8), ("cln_g", 8), ("cln_b", 8), ("b_pw2", 8),
               ("lmg0", 8), ("lmb0", 8), ("lfg0", 8), ("lfb0", 8), ("lmg1", 8), ("lmb1", 8),
               ("lfg1", 8), ("lfb1", 8), ("b_kd", 2), ("b_q", 8), ("b_o", 8), ("mask", 1)]:
    PCOLS[_n] = _off
    _off += _w
NPC = _off
DCOLS = {"albm0": 0, "albf0": 8, "albm1": 16, "bq8": 24, "agm0": 32, "agf0": 40, "agm1": 48, "cbo": 56}
NDC = 64


def _colvec(v):
    return np.ascontiguousarray(v.reshape(-1, 128).T)


def pack_params(inp, core):
    p = np.zeros((128, NPC), np.float32)

    def put(name, arr):
        p[:, PCOLS[name]:PCOLS[name] + arr.shape[1]] = arr
    put("b_pw1", _colvec(inp["conv_b_pw1"][0]))
    wd = inp["conv_w_dw"][0]
    put("w_dw", np.ascontiguousarray(wd.T.reshape(8, 128, CW).transpose(1, 0, 2).reshape(128, 8 * CW)))
    put("b_dw", _colvec(inp["conv_b_dw"][0]))
    put("cln_g", _colvec(inp["conv_ln_g"][0]))
    put("cln_b", _colvec(inp["conv_ln_b"][0]))
    put("b_pw2", _colvec(inp["conv_b_pw2"][0]))
    for l in range(2):
        put(f"lmg{l}", _colvec(inp["ln_mix_g"][l]))
        put(f"lmb{l}", _colvec(inp["ln_mix_b"][l]))
        put(f"lfg{l}", _colvec(inp["ln_ffn_g"][l]))
        put(f"lfb{l}", _colvec(inp["ln_ffn_b"][l]))
    bk = inp["kv_b_k"].reshape(2, 64)
    put("b_kd", np.ascontiguousarray(np.concatenate([bk, bk], axis=1).T))
    put("b_q", _colvec(inp["attn_b_q"][0]))
    put("b_o", _colvec(inp["attn_b_o"][0]))
    put("mask", np.full((128, 1), 0.0 if core == 0 else 1.0, np.float32))
    return p


def alibi_factor():
    i = np.arange(128)[None, :]
    j = np.arange(128)[:, None]
    slopes = np.exp2(-8.0 * np.arange(1, NH + 1, dtype=np.float64) / NH)
    A = np.zeros((128, NH, 2, 128), np.float64)
    d_prev = i + 128 - j
    d_cur = i - j
    for h in range(NH):
        A[:, h, 0, :] = np.where((d_prev >= 0) & (d_prev < 128), np.exp(-slopes[h] * d_prev), 0.0)
        A[:, h, 1, :] = np.where((d_cur >= 0) & (d_cur < 128), np.exp(-slopes[h] * d_cur), 0.0)
    return A.reshape(128, NH * 256).astype(np.float32)


class _Stop(Exception):
    pass


def build_nc(debug_stage=None):
    nc = bass.Bass("TRN2", target_bir_lowering=False)

    def chk(name):
        if debug_stage == name:
            raise _Stop()

    def din(name, shape):
        return nc.dram_tensor(name, list(shape), F32, kind="ExternalInput").ap()

    xT = din("xT", [D, XTOT])
    params_d = din("params", [128, NPC])
    bv_d = din("bv_bc", [128, 128])
    sink_d = din("sinks_bc", [128, NH])
    ident_d = din("ident", [128, 128])
    afac_d = din("afac", [128, NH * 256])
    w_pw1 = din("conv_w_pw1", [1, D, 2 * D])
    w_pw2 = din("conv_w_pw2", [1, D, D])
    w_k = din("kv_w_k", [D, 128])
    w_v = din("kv_w_v", [D, 128])
    w_q = din("attn_w_q", [1, D, D])
    w_o = din("attn_w_o", [1, D, D])
    w_g = din("ffn_w_gate", [2, D, FF])
    w_u = din("ffn_w_up", [2, D, FF])
    w_d = din("ffn_w_down", [2, FF, D])
    outT = nc.dram_tensor("outT", [D, TPC], F32, kind="ExternalOutput").ap()

    def kview(w2d):
        return w2d.rearrange("(kc p) n -> p kc n", p=128)

    with contextlib.ExitStack() as st:
        def sb(name, shape, dt):
            return st.enter_context(nc.sbuf_tensor(name, list(shape), dt))

        S = sb("S", [128, NCH, T0], F32)
        xb = sb("xb", [128, NCH, XC], BF16)
        zsq = sb("zsq", [128, NCH, T0], BF16)
        R = sb("R", [128, 25344], BF16)
        wring = [sb(f"wr{i}", [128, SLOTC], BF16) for i in range(NSLOT)]
        params = sb("params_sb", [128, NPC], F32)
        dpar = sb("dpar", [128, NDC], F32)
        ident = sb("ident_sb", [128, 128], BF16)
        onesD = sb("onesD", [128, 128], BF16)
        epsc = sb("epsc", [128, 1], F32)
        lndummy = sb("lndummy", [128, 1], F32)
        afac = sb("afac_sb", [128, NH * 256], BF16)
        bv_bc = sb("bv_sb", [128, 128], F32)
        esink = sb("esink", [128, NH + 2], F32)
        KT = sb("KT", [128, 2, T0], BF16)
        Vaug = sb("Vaug", [128, 9, 2, 65], BF16)
        htail = sb("htail", [128, NCH, CH], BF16)
        tmpf = [sb(f"tmpf{i}", [128, 512], F32) for i in range(2)]
        lnt = [None if i == 1 else sb(f"lnt{i}", [128, 512], F32) for i in range(5)]
        ebt = [sb(f"eb{i}", [128, 512], BF16) for i in range(2)]
        ptt = [sb(f"pt{i}", [128, 512], BF16) for i in range(4)]
        otok = [sb(f"otok{i}", [128, D + 128], BF16) for i in range(2)]
        rs_t = sb("rs_t", [128, 2, NH + 2], F32)
        ps = [st.enter_context(nc.psum_tensor(f"ps{i}", [128, 512], F32)) for i in range(3)]
        psO = st.enter_context(nc.psum_tensor("psO", [128, 3, 512], F32))
        ps += [psO[:, i, :] for i in range(3)]
        ps += [st.enter_context(nc.psum_tensor(f"ps{i}", [128, 512], F32)) for i in range(6, 8)]

        Rb = R[:]
        h_v = Rb[:, 0:NCH * XC].rearrange("p (c t) -> p c t", c=NCH)
        hs_v = Rb[:, NCH * XC:NCH * XC + NCH * T0].rearrange("p (c t) -> p c t", c=NCH)
        DG0 = NCH * XC + NCH * T0
        dg_v = [Rb[:, DG0:DG0 + CW * 128].rearrange("p (k m) -> p k m", k=CW),
                afac[:, 0:CW * 128].rearrange("p (k m) -> p k m", k=CW)]
        hid_v = Rb[:, 0:NFC * T0].rearrange("p (c t) -> p c t", c=NFC)
        qT_v = Rb[:, 0:NCH * T1].rearrange("p (c t) -> p c t", c=NCH)
        OT_v = Rb[:, NCH * T1:2 * NCH * T1].rearrange("p (c t) -> p c t", c=NCH)
        qTb_v = Rb[:, 2 * NCH * T1:3 * NCH * T1].rearrange("p (c t) -> p c t", c=NCH)

        P = Prog(nc)

        NT = T0 // 128
        tS = [[Tile() for _ in range(NT)] for _ in range(NCH)]
        tX = [[Tile() for _ in range(NT + 1)] for _ in range(NCH)]
        tQ = [[Tile() for _ in range(NT)] for _ in range(NCH)]
        tR = [[Tile() for _ in range(NT + 1)] for _ in range(NFC)]
        RT = 256
        nRt = 50688 // RT
        tRb = [Tile() for _ in range(nRt)]

        def rtiles(b0, b1):
            return tRb[b0 // RT:(b1 + RT - 1) // RT]

        def t_h(c, col0, ncol):
            b0 = (c * XC + col0) * 2
            return rtiles(b0, b0 + ncol * 2)

        def t_hs(c, t0, n):
            b0 = (NCH * XC + c * T0 + t0) * 2
            return rtiles(b0, b0 + n * 2)

        def t_dg(i):
            if i == 1:
                return [t_afac]
            b0 = (NCH * XC + NCH * T0) * 2
            return rtiles(b0, b0 + CW * 128 * 2) + [t_dgb[0]]

        def t_hid(c, t0, n):
            b0 = (c * T0 + t0) * 2
            return rtiles(b0, b0 + n * 2)

        def t_qT(c, t0, n):
            b0 = (c * T1 + t0) * 2
            return rtiles(b0, b0 + n * 2)

        def t_qTb(c, t0, n):
            b0 = (2 * NCH * T1 + c * T1 + t0) * 2
            return rtiles(b0, b0 + n * 2)

        def t_OT(c, t0, n):
            b0 = (NCH * T1 + c * T1 + t0) * 2
            return rtiles(b0, b0 + n * 2)

        def tl(arr, c, t0, n):
            return arr[c][t0 // 128:(t0 + n - 1) // 128 + 1]

        def tlx(c, col0, ncol):
            res = []
            if col0 < CH:
                res.append(tX[c][0])
            a = max(col0 - CH, 0)
            b = col0 + ncol - CH
            if b > a:
                res += tX[c][1 + a // 128:1 + (b - 1) // 128 + 1]
            return res

        t_par, t_dpar, t_ident, t_ones, t_afac, t_bv, t_esink, t_htail = [Tile() for _ in range(8)]
        t_dgb = [Tile(), Tile()]
        t_lndummy = Tile()
        t_dg2 = [Tile(), Tile()]
        t_KT = [[Tile() for _ in range(NT)] for _ in range(2)]
        t_V = [Tile() for _ in range(9)]
        t_Vones = Tile()
        t_wr = [[Tile(), Tile(), Tile(), Tile()] for _ in range(NSLOT)]
        t_tmpf = [Tile() for _ in range(2)]
        t_lnt = [Tile() for _ in range(5)]
        t_eb = [Tile() for _ in range(2)]
        t_pt = [Tile() for _ in range(4)]
        t_otok = [Tile() for _ in range(2)]
        t_otok3 = [[Tile() for _ in range(3)] for _ in range(2)]
        t_rs = [Tile() for _ in range(2)]
        t_ost = [Tile() for _ in range(3)]
        t_ps = [Tile() for _ in range(8)]

        cnt = {"bank": 0, "slot": 0, "tmpf": 0, "ost": 0, "eb": 0, "pt": 0, "ln": 0}

        def bank(lo=0, hi=8):
            b = lo + cnt["bank"] % (hi - lo)
            cnt["bank"] += 1
            return b

        def rot(name, n):
            v = cnt[name] % n
            cnt[name] += 1
            return v

        def pcol(name, c=0):
            o = PCOLS[name] + c
            return params[:, o:o + 1]

        def dcol(name, c=0):
            o = DCOLS[name] + c
            return dpar[:, o:o + 1]

        P.op("sp", lambda e: e.dma_start(out=params[:], in_=params_d), writes=[t_par], dma_sem="c0")
        P.op("sp", lambda e: e.dma_start(out=bv_bc[:], in_=bv_d), writes=[t_bv], dma_sem="c1")
        P.op("dve", lambda e: e.memset(esink[:], 0.0), writes=[t_esink])
        P.op("sp", lambda e: e.dma_start(out=esink[:, 0:NH], in_=sink_d), writes=[t_esink], dma_sem="c2")
        P.op("dve", lambda e: e.memset(onesD[:], 1.0 / D), writes=[t_ones])
        P.op("dve", lambda e: e.memset(epsc[:], EPS), writes=[t_ones])
        P.op("dve", lambda e: e.memset(Vaug[:, :, :, 64:65], 1.0), writes=[t_Vones])
        P.op("act", lambda e: e.activation(out=esink[:], in_=esink[:], func=AF.Exp), reads=[t_esink], writes=[t_esink])
        for nm, src, sc in [("albm0", "lmb0", ALPHA), ("albf0", "lfb0", ALPHA), ("albm1", "lmb1", ALPHA), ("bq8", "b_q", 0.125),
                            ("agm0", "lmg0", ALPHA), ("agf0", "lfg0", ALPHA), ("agm1", "lmg1", ALPHA)]:
            P.op("dve", lambda e, nm=nm, src=src, sc=sc: e.tensor_scalar(
                out=dpar[:, DCOLS[nm]:DCOLS[nm] + 8], in0=params[:, PCOLS[src]:PCOLS[src] + 8],
                scalar1=sc, scalar2=None, op0=ALU.mult), reads=[t_par], writes=[t_dpar])
        P.op("dve", lambda e: e.tensor_tensor(out=dpar[:, DCOLS["cbo"]:DCOLS["cbo"] + 8], in0=params[:, PCOLS["b_o"]:PCOLS["b_o"] + 8],
                                              in1=dpar[:, DCOLS["albf0"]:DCOLS["albf0"] + 8], op=ALU.add),
             reads=[t_par, t_dpar], writes=[t_dpar])

        WL = []
        wstate = {"issued": 0, "next": 0, "held": []}
        AHEAD = 2

        def plan(sbi_):
            def std(c0, kc, src, ti):
                return (ti, c0, kc, 128, 0, 128, src)
            wp1 = kview(w_pw1[0]); wp2 = kview(w_pw2[0]); wqv = kview(w_q[0]); wov = kview(w_o[0])
            for oc in range(NCH):
                WL.append([std(0, 8, wp1[:, :, oc * 128:(oc + 1) * 128], 0),
                           std(1024, 8, wp1[:, :, D + oc * 128:D + (oc + 1) * 128], 1)])
            for oc in range(NCH):
                WL.append([std(0, 8, wp2[:, :, oc * 128:(oc + 1) * 128], 0)])

            def ffn_plan(layer):
                wg = kview(w_g[layer]); wu = kview(w_u[layer]); wd = kview(w_d[layer])
                for fc in range(NFC):
                    WL.append([std(0, 8, wg[:, :, fc * 128:(fc + 1) * 128], 0),
                               std(1024, 8, wu[:, :, fc * 128:(fc + 1) * 128], 1)])
                for oc in range(NCH):
                    WL.append([std(0, 11, wd[:, 0:11, oc * 128:(oc + 1) * 128], 0),
                               std(1408, 11, wd[:, 11:22, oc * 128:(oc + 1) * 128], 1)])
            ffn_plan(0)
            wkv = kview(w_k); wvv = kview(w_v)
            kvp = []
            for kvh in range(2):
                for dup in range(2):
                    kvp.append((2 * kvh + dup, kvh * 1024, 8, 128, dup * 64, dup * 64 + 64, wkv[:, :, kvh * 64:(kvh + 1) * 64]))
            WL.append(kvp)
            WL.append([std(0, 8, wvv, 0)])
            for oc in range(NCH):
                WL.append([std(0, 8, wqv[:, :, oc * 128:(oc + 1) * 128], 0)])
            for oc in range(NCH):
                WL.append([std(0, 8, wov[:, :, oc * 128:(oc + 1) * 128], 0)])
            ffn_plan(1)

        def whold():
            wstate["held"].append(wstate["next"])

        def wrelease():
            wstate["held"].pop(0)

        def _issue(i):
            assert not wstate["held"] or i - NSLOT < min(wstate["held"]), ("weight slot reuse before consumers recorded", i, wstate["held"])
            s = i % NSLOT
            extra = set()
            for t in t_wr[s]:
                extra.update(t.r)
                t.r = []
            for (ti, c0, kc, n, lo, hi, src_ap) in WL[i]:
                dst = wring[s][:, c0:c0 + kc * n].rearrange("p (k n) -> p k n", k=kc)[:, :, lo:hi]

                P.op("pool", lambda e, dst=dst, src_ap=src_ap: e.dma_start(out=dst, in_=src_ap),
                     writes=[t_wr[s][ti]], dma_sem=f"w{s}_{ti}", extra=set(extra))

        def wnext(ahead=AHEAD):
            i = wstate["next"]
            wstate["next"] += 1
            while wstate["issued"] < min(len(WL), i + 1 + ahead):
                _issue(wstate["issued"])
                wstate["issued"] += 1
            s = i % NSLOT
            tis = sorted({p[0] for p in WL[i]})
            return s, [t_wr[s][ti] for ti in tis]

        def wprefetch(k):
            while wstate["issued"] < min(len(WL), wstate["next"] + k):
                _issue(wstate["issued"])
                wstate["issued"] += 1

        def wv(s, c0, kc, n=128):
            return wring[s][:, c0:c0 + kc * n].rearrange("p (k n) -> p k n", k=kc)

        def ln_stats(zb_ap, zb_tiles, sq_ap, sq_tiles, n):
            pm = bank()
            pq = bank()
            for kc in range(NCH):
                P.op("pe", lambda e, kc=kc, pm=pm: e.matmul(ps[pm][:, 0:n], lhsT=onesD[:], rhs=zb_ap[:, kc, :],
                                                            start=(kc == 0), stop=(kc == NCH - 1)),
                     reads=[t_ones] + zb_tiles[kc], writes=[t_ps[pm]], pe_acc=(kc > 0))
            for kc in range(NCH):
                P.op("pe", lambda e, kc=kc, pq=pq: e.matmul(ps[pq][:, 0:n], lhsT=onesD[:], rhs=sq_ap[:, kc, :],
                                                            start=(kc == 0), stop=(kc == NCH - 1)),
                     reads=[t_ones] + sq_tiles[kc], writes=[t_ps[pq]], pe_acc=(kc > 0))
            tfi = rot("tmpf", 2)
            par = rot("ln", 2)
            mi, ri = (0, 2) if par == 0 else (3, 4)
            P.op("act", lambda e: e.activation(out=lndummy[:, 0:1], in_=epsc[:, 0:1], func=AF.Ln), reads=[t_ones], writes=[t_lndummy])
            P.op("act", lambda e: e.activation(out=lnt[mi][:, 0:n], in_=ps[pm][:, 0:n], func=AF.Copy),
                 reads=[t_ps[pm]], writes=[t_lnt[mi]])
            P.op("dve", lambda e: e.tensor_tensor(out=tmpf[tfi][:, 0:n], in0=lnt[mi][:, 0:n], in1=lnt[mi][:, 0:n], op=ALU.mult),
                 reads=[t_lnt[mi]], writes=[t_tmpf[tfi]])
            P.op("dve", lambda e: e.tensor_tensor(out=tmpf[tfi][:, 0:n], in0=ps[pq][:, 0:n], in1=tmpf[tfi][:, 0:n], op=ALU.subtract),
                 reads=[t_ps[pq], t_tmpf[tfi]], writes=[t_tmpf[tfi]])
            P.op("act", lambda e: e.activation(out=tmpf[tfi][:, 0:n], in_=tmpf[tfi][:, 0:n], func=AF.Ln, bias=epsc[:, 0:1], scale=1.0),
                 reads=[t_tmpf[tfi], t_ones], writes=[t_tmpf[tfi]])
            P.op("act", lambda e: e.activation(out=lnt[ri][:, 0:n], in_=tmpf[tfi][:, 0:n], func=AF.Exp, scale=-0.5),
                 reads=[t_tmpf[tfi]], writes=[t_lnt[ri]])
            return mi, ri

        AGN = {"lmg0": "agm0", "lfg0": "agf0", "lmg1": "agm1"}
        pending = []

        def drip(k=1):
            for _ in range(min(k, len(pending))):
                pending.pop(0)()

        def flush():
            drip(len(pending))

        def ln_stream(t0, n, gname, bname, albname, final=False, out_t0=None, defer=False):
            def body():
                P.phase = f"lnstats{t0}"
                zb_ap = xb[:, :, CH + t0:CH + t0 + n]
                sq_ap = zsq[:, :, t0:t0 + n]
                mi, ri = ln_stats(zb_ap, [tlx(c, CH + t0, n) for c in range(NCH)], sq_ap, [tl(tQ, c, t0, n) for c in range(NCH)], n)
                for c in range(NCH):
                    sv = S[:, c, t0:t0 + n]
                    st_ = tl(tS, c, t0, n)
                    P.op("pool" if c in (2, 5, 7) else "dve", lambda e, sv=sv: e.tensor_tensor(out=sv, in0=sv, in1=lnt[mi][:, 0:n], op=ALU.subtract),
                         reads=st_ + [t_lnt[mi]], writes=st_)
                    gcol = pcol(gname, c) if final else dcol(AGN[gname], c)
                    P.op("dve", lambda e, sv=sv, gcol=gcol: e.scalar_tensor_tensor(out=sv, in0=sv, scalar=gcol, in1=lnt[ri][:, 0:n],
                                                                               op0=ALU.mult, op1=ALU.mult),
                         reads=st_ + [t_lnt[ri], t_par, t_dpar], writes=st_)
                    if final:
                        P.op("act", lambda e, sv=sv, c=c: e.activation(out=sv, in_=sv, func=AF.Identity, bias=pcol(bname, c), scale=1.0),
                             reads=st_ + [t_par], writes=st_)
                        P.op("sp", lambda e, sv=sv, c=c: e.dma_start(out=outT[c * 128:(c + 1) * 128, out_t0:out_t0 + n], in_=sv),
                             reads=st_, dma_sem=f"o{c}")
                    else:
                        P.op("act", lambda e, sv=sv, c=c: e.activation(out=xb[:, c, CH + t0:CH + t0 + n], in_=sv, func=AF.Identity,
                                                                       bias=pcol(bname, c), scale=1.0 / ALPHA),
                             reads=st_ + [t_par], writes=tlx(c, CH + t0, n))

            if defer and DEFER_LN:
                pending.append(body)
            else:
                flush()
                body()

        def resid_evac(pb, c, t0, n, bias_ap):
            sv = S[:, c, t0:t0 + n]
            st_ = tl(tS, c, t0, n)
            if bias_ap is not None:
                P.op("dve", lambda e: e.scalar_tensor_tensor(out=sv, in0=ps[pb][:, 0:n], scalar=bias_ap, in1=sv,
                                                             op0=ALU.add, op1=ALU.add),
                     reads=[t_ps[pb], t_par, t_dpar] + st_, writes=st_)
            else:
                P.op("dve", lambda e: e.tensor_tensor(out=sv, in0=ps[pb][:, 0:n], in1=sv, op=ALU.add),
                     reads=[t_ps[pb]] + st_, writes=st_)
            P.op("act", lambda e: e.activation(out=xb[:, c, CH + t0:CH + t0 + n], in_=sv, func=AF.Copy),
                 reads=st_, writes=tlx(c, CH + t0, n))
            P.op("dve" if c == NCH - 1 else "pool", lambda e: e.tensor_tensor(out=zsq[:, c, t0:t0 + n], in0=sv, in1=sv, op=ALU.mult),
                 reads=st_, writes=tl(tQ, c, t0, n))

        def proj_resid(wsrc, rhs_ap_fn, rhs_tiles_fn, nk, blocks, bias_name, ln_cb=None):
            P.phase = f"resid_nk{nk}"
            flush()
            ws = []
            whold()
            first = wstate["next"]
            for oc in range(NCH):
                s, wt = wnext(ahead=NCH - 1 - oc + (NSLOT - NCH))
                ws.append((wv(s, 0, nk), wt))
            prev = None
            for (t0, n) in blocks:
                for oc in range(NCH):
                    w, wt = ws[oc]
                    pb = bank()
                    for kc in range(nk):
                        P.op("pe", lambda e, kc=kc, pb=pb, t0=t0, n=n, w=w: e.matmul(ps[pb][:, 0:n], lhsT=w[:, kc, :], rhs=rhs_ap_fn(kc, t0, n),
                                                                                 start=(kc == 0), stop=(kc == nk - 1)),
                             reads=wt + rhs_tiles_fn(kc, t0, n), writes=[t_ps[pb]], pe_acc=(kc > 0))
                    resid_evac(pb, oc, t0, n, bias_name(oc) if callable(bias_name) else (pcol(bias_name, oc) if bias_name else None))
                    if oc == 0 and prev is not None:
                        if ln_cb is not None:
                            ln_cb(prev[0], prev[1], False)
                        prev = None
                    if (t0, n) == blocks[-1]:
                        wstate["held"][0] = first + oc + 1
                        if oc < AHEAD:
                            wprefetch(oc + 1)
                if (t0, n) == blocks[-1]:
                    wrelease()
                    wprefetch(AHEAD)
                    if ln_cb is not None:
                        ln_cb(t0, n, True)
                else:
                    prev = (t0, n)

        def ffn(layer, blocks, gname, bname, albname, final, out_off, after_stats=None):
            wg = kview(w_g[layer])
            wu = kview(w_u[layer])
            wd = kview(w_d[layer])
            def gu(fc, wgv, wuv, wt, t0, n):
                pg = bank()
                pu = bank()
                for kc in range(NCH):
                    P.op("pe", lambda e, kc=kc: e.matmul(ps[pg][:, 0:n], lhsT=wgv[:, kc, :], rhs=xb[:, kc, CH + t0:CH + t0 + n],
                                                         start=(kc == 0), stop=(kc == NCH - 1)),
                         reads=[wt[0]] + tlx(kc, CH + t0, n), writes=[t_ps[pg]], pe_acc=(kc > 0))
                for kc in range(NCH):
                    P.op("pe", lambda e, kc=kc: e.matmul(ps[pu][:, 0:n], lhsT=wuv[:, kc, :], rhs=xb[:, kc, CH + t0:CH + t0 + n],
                                                         start=(kc == 0), stop=(kc == NCH - 1)),
                         reads=[wt[1]] + tlx(kc, CH + t0, n), writes=[t_ps[pu]], pe_acc=(kc > 0))
                tf = rot("tmpf", 2)
                P.op("act", lambda e: e.activation(out=tmpf[tf][:, 0:n], in_=ps[pg][:, 0:n], func=AF.Silu),
                     reads=[t_ps[pg]], writes=[t_tmpf[tf]])
                P.op("dve", lambda e: e.tensor_tensor(out=hid_v[:, fc, t0:t0 + n], in0=tmpf[tf][:, 0:n], in1=ps[pu][:, 0:n], op=ALU.mult),
                     reads=[t_ps[pu], t_tmpf[tf]], writes=t_hid(fc, t0, n))

            P.phase = f"ffn_gu"
            NP = 4 if len(blocks) >= 3 else 5
            pro = []
            whold()
            for fc in range(NP):
                s, wt = wnext()
                pro.append((fc, wv(s, 0, 8), wv(s, 1024, 8), wt))
            for (fc, wgv, wuv, wt) in pro:
                for (t0, n) in blocks[:-1]:
                    gu(fc, wgv, wuv, wt, t0, n)
                    drip(3)
            flush()
            for (fc, wgv, wuv, wt) in pro:
                gu(fc, wgv, wuv, wt, *blocks[-1])
            wrelease()
            for fc in range(NP, NFC):
                s, wt = wnext()
                wgv = wv(s, 0, 8)
                wuv = wv(s, 1024, 8)
                for (t0, n) in blocks:
                    gu(fc, wgv, wuv, wt, t0, n)
            def cb(t0, n, last):
                ln_stream(t0, n, gname, bname, albname, final=final, out_t0=(t0 + out_off) if final else None, defer=(last and not final))
                if after_stats is not None:
                    after_stats(t0, n)
            proj_resid(wd, lambda kc, t0, n: hid_v[:, kc, t0:t0 + n], lambda kc, t0, n: t_hid(kc, t0, n), NFC, blocks,
                       (lambda oc: dcol("albm0" if layer == 0 else "albm1", oc)), ln_cb=cb)

        def conv_module(sbi, blocks, T, load_S):
            P.phase = f"pw1_sb{sbi}"
            wp1 = kview(w_pw1[0])
            def build_diag(c, di):
                NDV = CW
                ex = set()
                for t in t_dg(di) + [t_dg2[di]]:
                    ex.update(t.r)
                    if t.w is not None:
                        ex.add(t.w)
                for k in range(CW):
                    wcol = params[:, PCOLS["w_dw"] + c * CW + k:PCOLS["w_dw"] + c * CW + k + 1]
                    if k < NDV:
                        P.op("dve", lambda e, di=di, k=k, wcol=wcol: e.tensor_scalar(out=dg_v[di][:, k, :], in0=ident[:], scalar1=wcol,
                                                                                 scalar2=None, op0=ALU.mult),
                             reads=[t_ident, t_par], writes=(t_dg(di) if k == NDV - 1 else []), extra=(set(ex) if k == 0 else None))
                    else:
                        P.op("act", lambda e, di=di, k=k, wcol=wcol: e.activation(out=dg_v[di][:, k, :], in_=ident[:], func=AF.Copy, scale=wcol),
                             reads=[t_ident, t_par], writes=([t_dg2[di]] if k == CW - 1 else []), extra=(set(ex) if k == NDV else None))

            P.phase = f"conv_sb{sbi}"
            if sbi == 1:
                P.op("pool", lambda e: e.tensor_copy(out=h_v[:, :, 0:CH], in_=htail[:]), reads=[t_htail],
                     writes=[t for c in range(NCH) for t in t_h(c, 0, CH)])
            pblocks = list(blocks)
            if sbi == 0:
                pblocks = [(-CH, CH + blocks[0][1])] + pblocks[1:]
            for oc in range(NCH):
                if oc == 3:
                    build_diag(0, 0)
                s, wt = wnext()
                wa = wv(s, 0, 8)
                wgt = wv(s, 1024, 8)
                for (t0, n) in pblocks:
                    pa = bank()
                    pg = bank()
                    for kc in range(NCH):
                        P.op("pe", lambda e, kc=kc, pa=pa, t0=t0, n=n, wa=wa: e.matmul(ps[pa][:, 0:n], lhsT=wa[:, kc, :],
                                                                                    rhs=xb[:, kc, CH + t0:CH + t0 + n],
                                                                                    start=(kc == 0), stop=(kc == NCH - 1)),
                             reads=[wt[0]] + tlx(kc, CH + t0, n), writes=[t_ps[pa]], pe_acc=(kc > 0))
                    for kc in range(NCH):
                        P.op("pe", lambda e, kc=kc, pg=pg, t0=t0, n=n, wgt=wgt: e.matmul(ps[pg][:, 0:n], lhsT=wgt[:, kc, :],
                                                                                      rhs=xb[:, kc, CH + t0:CH + t0 + n],
                                                                                      start=(kc == 0), stop=(kc == NCH - 1)),
                             reads=[wt[1]] + tlx(kc, CH + t0, n), writes=[t_ps[pg]], pe_acc=(kc > 0))
                    tf = rot("tmpf", 2)
                    P.op("act", lambda e, pg=pg, tf=tf, n=n, oc=oc: e.activation(out=tmpf[tf][:, 0:n], in_=ps[pg][:, 0:n], func=AF.Sigmoid,
                                                                              bias=pcol("b_pw1", 8 + oc), scale=1.0),
                         reads=[t_ps[pg], t_par], writes=[t_tmpf[tf]])
                    hdst = h_v[:, oc, CH + t0:CH + t0 + n]
                    glu_op = P.op("dve", lambda e, pa=pa, tf=tf, n=n, oc=oc, hdst=hdst: e.scalar_tensor_tensor(
                        out=hdst, in0=ps[pa][:, 0:n], scalar=pcol("b_pw1", oc), in1=tmpf[tf][:, 0:n], op0=ALU.add, op1=ALU.mult),
                        reads=[t_ps[pa], t_tmpf[tf], t_par], writes=t_h(oc, CH + t0, n))
                    if oc == 0 and (t0, n) == pblocks[0]:
                        load_S(extra={("eng", "dve", glu_op.idx)})
                    if t0 < 0:
                        hhalo = h_v[:, oc, 0:CH]
                        P.op("dve", lambda e, hhalo=hhalo: e.tensor_scalar(out=hhalo, in0=hhalo, scalar1=pcol("mask"), scalar2=None, op0=ALU.mult),
                             reads=t_h(oc, 0, CH) + [t_par], writes=t_h(oc, 0, CH))
                    drip(1)
            if sbi == 0:
                P.op("pool", lambda e: e.tensor_copy(out=htail[:], in_=h_v[:, :, T:T + CH]),
                     reads=[t for c in range(NCH) for t in t_h(c, T, CH)], writes=[t_htail])
            chk("pw1")
            flush()
            for c in range(NCH):
                for (t0, n) in blocks:
                    P.op("act", lambda e, c=c, t0=t0, n=n: e.activation(out=S[:, c, t0:t0 + n], in_=S[:, c, t0:t0 + n], func=AF.Copy, scale=ALPHA),
                         reads=tl(tS, c, t0, n), writes=tl(tS, c, t0, n))
            def conv_ln(t0, n):
                zb_ap = xb[:, :, CH + t0:CH + t0 + n]
                mi, ri = ln_stats(zb_ap, [tlx(c, CH + t0, n) for c in range(NCH)], zsq[:, :, t0:t0 + n],
                         [tl(tQ, c, t0, n) for c in range(NCH)], n)
                P.op("act", lambda e: e.activation(out=lndummy[:, 0:1], in_=epsc[:, 0:1], func=AF.Silu), reads=[t_ones], writes=[t_lndummy])
                for c in range(NCH):
                    tf = rot("tmpf", 2)
                    P.op("pool" if c % 4 == 3 else "dve", lambda e, c=c, tf=tf, t0=t0, n=n, mi=mi: e.tensor_tensor(out=tmpf[tf][:, 0:n], in0=xb[:, c, CH + t0:CH + t0 + n],
                                                                                 in1=lnt[mi][:, 0:n], op=ALU.subtract),
                         reads=tlx(c, CH + t0, n) + [t_lnt[mi]], writes=[t_tmpf[tf]])
                    P.op("dve", lambda e, c=c, tf=tf, n=n, ri=ri: e.scalar_tensor_tensor(out=tmpf[tf][:, 0:n], in0=tmpf[tf][:, 0:n], scalar=pcol("cln_g", c),
                                                                                in1=lnt[ri][:, 0:n], op0=ALU.mult, op1=ALU.mult),
                         reads=[t_tmpf[tf], t_lnt[ri], t_par], writes=[t_tmpf[tf]])
                    P.op("act", lambda e, c=c, tf=tf, t0=t0, n=n: e.activation(out=hs_v[:, c, t0:t0 + n], in_=tmpf[tf][:, 0:n], func=AF.Silu,
                                                                             bias=pcol("cln_b", c), scale=1.0),
                         reads=[t_tmpf[tf], t_par], writes=t_hs(c, t0, n))

            wprefetch(NCH)
            groups = [list(blocks)]
            dcount = [0]

            def conv_block(c, di, t0, n):
                pc = bank()
                for k in range(CW):
                    P.op("pe", lambda e, k=k: e.matmul(ps[pc][:, 0:n], lhsT=dg_v[di][:, k, :], rhs=h_v[:, c, t0 + k:t0 + k + n],
                                                       start=(k == 0), stop=(k == CW - 1)),
                         reads=((t_dg(di) + [t_dg2[di]]) if k in (0, CW - 1) else []) + t_h(c, t0 + k, n), writes=[t_ps[pc]], pe_acc=(k > 0))
                P.op("act", lambda e: e.activation(out=xb[:, c, CH + t0:CH + t0 + n], in_=ps[pc][:, 0:n], func=AF.Identity,
                                                   bias=pcol("b_dw", c), scale=1.0),
                     reads=[t_ps[pc], t_par], writes=tlx(c, CH + t0, n))
                P.op("dve", lambda e: e.scalar_tensor_tensor(out=zsq[:, c, t0:t0 + n], in0=ps[pc][:, 0:n], scalar=pcol("b_dw", c),
                                                             in1=xb[:, c, CH + t0:CH + t0 + n], op0=ALU.add, op1=ALU.mult),
                     reads=[t_ps[pc], t_par] + tlx(c, CH + t0, n), writes=tl(tQ, c, t0, n))

            for gi, grp in enumerate(groups):
                if gi > 0:
                    build_diag(0, dcount[0] % 2)
                dis = {}
                for c in range(NCH - 2):
                    dis[c] = dcount[0] % 2
                    dcount[0] += 1
                    build_diag(c + 1, dcount[0] % 2)
                    for bi, (t0, n) in enumerate(grp):
                        conv_block(c, dis[c], t0, n)
                c6, c7 = NCH - 2, NCH - 1
                dis[c6] = dcount[0] % 2
                dcount[0] += 1
                build_diag(c7, dcount[0] % 2)
                dis[c7] = dcount[0] % 2
                dcount[0] += 1
                for bi, (t0, n) in enumerate(grp):
                    conv_block(c6, dis[c6], t0, n)
                    conv_block(c7, dis[c7], t0, n)
                    conv_ln(t0, n)
            P.op("pool", lambda e: e.dma_start(out=afac[:], in_=afac_d), writes=[t_afac], dma_sem="c4")
            chk("conv")
            chk("cln")
            proj_resid(kview(w_pw2[0]), lambda kc, t0, n: hs_v[:, kc, t0:t0 + n], lambda kc, t0, n: t_hs(kc, t0, n), NCH, blocks, "b_pw2",
                       ln_cb=lambda t0, n, last: ln_stream(t0, n, "lmg0", "lmb0", "albm0", defer=last))

        qs = {"prep": False, "done": set(), "w": {}}

        def q_reset():
            qs["prep"] = False
            qs["done"] = set()
            qs["w"] = {}

        def q_prep():
            P.op("dve", lambda e: e.memset(qT_v[64:128, :, :], 0.0), writes=[t for c in range(NCH) for t in t_qT(c, 0, T1)])
            P.op("dve", lambda e: e.memset(qTb_v[0:64, :, :], 0.0), writes=[t for c in range(NCH) for t in t_qTb(c, 0, T1)])
            qs["prep"] = True
            whold()

        def q_unit(oc, bi, blocks, base, ahead=None):
            P.phase = f"qproj"
            if oc not in qs["w"]:
                s, wt = wnext() if ahead is None else wnext(ahead=ahead)
                qs["w"][oc] = (wv(s, 0, 8), wt)
            w, wt = qs["w"][oc]
            t0, n = blocks[bi]
            pb = bank()
            for kc in range(NCH):
                P.op("pe", lambda e, kc=kc: e.matmul(ps[pb][:, 0:n], lhsT=w[:, kc, :], rhs=xb[:, kc, CH + t0:CH + t0 + n],
                                                     start=(kc == 0), stop=(kc == NCH - 1)),
                     reads=wt + tlx(kc, CH + t0, n), writes=[t_ps[pb]], pe_acc=(kc > 0))
            P.op("act", lambda e: e.activation(out=qT_v[0:64, oc, t0 - base:t0 - base + n], in_=ps[pb][0:64, 0:n],
                                               func=AF.Identity, bias=dcol("bq8", oc)[0:64, :], scale=0.125),
                 reads=[t_ps[pb], t_dpar], writes=t_qT(oc, t0 - base, n))
            P.op("act", lambda e: e.activation(out=qTb_v[64:128, oc, t0 - base:t0 - base + n], in_=ps[pb][64:128, 0:n],
                                               func=AF.Identity, bias=dcol("bq8", oc)[64:128, :], scale=0.125),
                 reads=[t_ps[pb], t_dpar], writes=t_qTb(oc, t0 - base, n))
            qs["done"].add((oc, bi))

        def kv_proj(sbi, blocks, T, before_last=None):
            koff = 0 if sbi == 0 else 128
            if sbi == 1:
                for kvh in range(2):
                    P.op("pool", lambda e, kvh=kvh: e.tensor_copy(out=KT[:, kvh, 0:128], in_=KT[:, kvh, 1024:1152]),
                         reads=[t_KT[kvh][8]], writes=[t_KT[kvh][0]])
                P.op("pool", lambda e: e.tensor_copy(out=Vaug[:, 0, :, 0:64], in_=Vaug[:, 8, :, 0:64]), reads=[t_V[8]], writes=[t_V[0]])
            P.phase = f"kv_sb{sbi}"
            whold()
            s, wt_k = wnext()
            s2, wt_v = wnext()
            wt_kv = [wt_k, wt_v[0]]
            wprefetch(4)
            wk_ = [wv(s, 0, 8), wv(s, 1024, 8)]
            wv_ = wv(s2, 0, 8)
            for (t0, n) in blocks:
                if (t0, n) == blocks[-1]:
                    if before_last is not None:
                        before_last()
                    flush()
                for kvh in range(2):
                    pb = bank()
                    for kc in range(NCH):
                        P.op("pe", lambda e, kc=kc, pb=pb, t0=t0, n=n, kvh=kvh: e.matmul(ps[pb][:, 0:n], lhsT=wk_[kvh][:, kc, :],
                                                                                      rhs=xb[:, kc, CH + t0:CH + t0 + n],
                                                                                      start=(kc == 0), stop=(kc == NCH - 1)),
                             reads=wt_kv[0][2 * kvh:2 * kvh + 2] + tlx(kc, CH + t0, n), writes=[t_ps[pb]], pe_acc=(kc > 0))
                    P.op("act", lambda e, pb=pb, t0=t0, n=n, kvh=kvh: e.activation(out=KT[:, kvh, koff + t0:koff + t0 + n], in_=ps[pb][:, 0:n],
                                                                                 func=AF.Identity, bias=pcol("b_kd", kvh), scale=1.0),
                         reads=[t_ps[pb], t_par], writes=t_KT[kvh][(koff + t0) // 128:(koff + t0 + n - 1) // 128 + 1])
                for j in range(n // 128):
                    tt = t0 + j * 128
                    vt = (koff + tt) // 128
                    pb = bank()
                    for kc in range(NCH):
                        P.op("pe", lambda e, kc=kc, pb=pb, tt=tt: e.matmul(ps[pb][:, 0:128], lhsT=xb[:, kc, CH + tt:CH + tt + 128],
                                                                         rhs=wv_[:, kc, :], start=(kc == 0), stop=(kc == NCH - 1)),
                             reads=[wt_kv[1]] + tlx(kc, CH + tt, 128), writes=[t_ps[pb]], pe_acc=(kc > 0))
                    P.op("dve", lambda e, pb=pb, vt=vt: e.tensor_tensor(out=Vaug[:, vt, :, 0:64],
                                                                      in0=ps[pb][:, 0:128].rearrange("p (h d) -> p h d", h=2),
                                                                      in1=bv_bc[:].rearrange("p (h d) -> p h d", h=2), op=ALU.add),
                         reads=[t_ps[pb], t_bv], writes=[t_V[vt]])
                if (t0, n) != blocks[-1]:
                    drip(3)
            wrelease()

        def attention(sbi, blocks):
            koff = 0 if sbi == 0 else 128
            base = blocks[0][0]
            wqv = kview(w_q[0])
            if not qs["prep"]:
                q_prep()
            for oc in range(NCH):
                if oc == 4:
                    wrelease()
                for bi in range(len(blocks)):
                    if (oc, bi) not in qs["done"]:
                        q_unit(oc, bi, blocks, base)
            P.phase = f"attcore"
            nqb = T1 // 128
            pend = None

            def qk(qb, oc):
                kvh = oc // 4
                q0 = qb * 128
                kc_cur = koff + base + q0
                pS = bank(0, 3)
                for hh in range(2):
                    for part in range(2):
                        kc0 = kc_cur - 128 + part * 128
                        P.op("pe", lambda e, hh=hh, part=part, kc0=kc0, pS=pS, kvh=kvh, q0=q0, oc=oc: e.matmul(
                            ps[pS][:, (hh * 2 + part) * 128:(hh * 2 + part + 1) * 128],
                            lhsT=KT[:, kvh, kc0:kc0 + 128],
                            rhs=(qT_v if hh == 0 else qTb_v)[:, oc, q0:q0 + 128], start=True, stop=True),
                            reads=[t_KT[kvh][kc0 // 128]] + (t_qT(oc, q0, 128) if hh == 0 else t_qTb(oc, q0, 128)), writes=[t_ps[pS]], pe_acc=(hh + part > 0))
                eb = rot("eb", 2)
                pt = rot("pt", 4)
                P.op("act", lambda e, pS=pS, eb=eb: e.activation(out=ebt[eb][:], in_=ps[pS][:], func=AF.Exp),
                     reads=[t_ps[pS]], writes=[t_eb[eb]])
                P.op("dve", lambda e, eb=eb, pt=pt, oc=oc: e.tensor_tensor(out=ptt[pt][:], in0=ebt[eb][:],
                                                                       in1=afac[:, oc * 512:(oc + 1) * 512], op=ALU.mult),
                     reads=[t_eb[eb], t_afac], writes=[t_pt[pt]])
                if sbi == 0 and qb == 0:
                    v = ptt[pt][:].rearrange("p (h a i) -> p h a i", h=2, a=2)[:, :, 0, :]
                    P.op("dve", lambda e, v=v: e.tensor_scalar(out=v, in0=v, scalar1=pcol("mask"), scalar2=None, op0=ALU.mult),
                         reads=[t_pt[pt], t_par], writes=[t_pt[pt]])
                return pt

            def obank(qb, g):
                if g == 0:
                    return 3 if qb % 2 == 0 else 7
                return 3 + g

            def pv(qb, oc, pt):
                kvh = oc // 4
                vt_cur = (koff + base + qb * 128) // 128
                for hh in range(2):
                    hgl = 2 * oc + hh
                    pb = obank(qb, hgl // 6)
                    c0 = (hgl % 6) * 65
                    for part in range(2):
                        vt = vt_cur - 1 + part
                        P.op("pe", lambda e, hh=hh, part=part, vt=vt, pb=pb, c0=c0, pt=pt, kvh=kvh: e.matmul(
                            ps[pb][:, c0:c0 + 65], lhsT=ptt[pt][:, (hh * 2 + part) * 128:(hh * 2 + part + 1) * 128],
                            rhs=Vaug[:, vt, kvh, :], start=(part == 0), stop=(part == 1)),
                            reads=[t_pt[pt], t_V[vt], t_Vones], writes=[t_ps[pb]], pe_acc=(part > 0))

            GRP = [(0, 6), (6, 12), (12, 16)]

            def norm_sums(qb):
                ob = qb % 2
                for g, (h0, h1) in enumerate(GRP):
                    pbk = obank(qb, g)
                    pv3 = ps[pbk][:, 0:(h1 - h0) * 65].rearrange("p (h d) -> p h d", d=65)
                    P.op("dve", lambda e, pv3=pv3, h0=h0, h1=h1: e.tensor_tensor(
                        out=rs_t[:, ob, h0:h1].unsqueeze(2), in0=pv3[:, :, 64:65], in1=esink[:, h0:h1].unsqueeze(2), op=ALU.add),
                        reads=[t_ps[pbk], t_esink], writes=[t_rs[ob]])
                P.op("dve", lambda e: e.reciprocal(out=rs_t[:, ob, 0:NH], in_=rs_t[:, ob, 0:NH]), reads=[t_rs[ob]], writes=[t_rs[ob]])

            def norm_mult(qb, g):
                ob = qb % 2
                h0, h1 = GRP[g]
                pbk = obank(qb, g)
                nh = h1 - h0
                pv3 = ps[pbk][:, 0:nh * 65].rearrange("p (h d) -> p h d", d=65)
                P.op("dve", lambda e: e.tensor_tensor(
                    out=otok[ob][:, h0 * 64:h1 * 64].rearrange("p (h d) -> p h d", d=64), in0=pv3[:, :, 0:64],
                    in1=rs_t[:, ob, h0:h1].unsqueeze(2).broadcast_to([128, nh, 64]), op=ALU.mult),
                    reads=[t_ps[pbk], t_rs[ob]], writes=[t_otok3[ob][g]])

            def transposes(qb, copy_half=None):
                ob = qb % 2
                pb = 6
                pbv = ps[pb][:].bitcast(BF16)
                q0 = qb * 128
                if copy_half in (None, 0):
                    for c in range(NCH):
                        P.op("pe", lambda e, c=c: e.transpose(out=pbv[:, c * 128:(c + 1) * 128],
                                                              in_=otok[ob][:, c * 128:(c + 1) * 128], identity=ident[:]),
                             reads=t_otok3[ob] + [t_ident], writes=[t_ps[pb]], pe_acc=(c > 0))
                halves = [0, 1] if copy_half is None else [copy_half]
                for hf in halves:
                    c0 = hf * 4
                    P.op("act", lambda e, c0=c0: e.activation(out=OT_v[:, c0:c0 + 4, q0:q0 + 128],
                                                             in_=pbv[:, c0 * 128:(c0 + 4) * 128].rearrange("p (c t) -> p c t", c=4), func=AF.Copy),
                         reads=[t_ps[pb]], writes=[t for c in range(c0, c0 + 4) for t in t_OT(c, q0, 128)])

            seq = [(qb, oc) for qb in range(nqb) for oc in range(NCH)]
            LA = 3
            pts = {i: qk(*seq[i]) for i in range(LA)}
            for idx, (qb, oc) in enumerate(seq):
                if idx + LA < len(seq):
                    pts[idx + LA] = qk(*seq[idx + LA])
                pv(qb, oc, pts.pop(idx))
                if qb >= 1:
                    if oc == 0:
                        norm_sums(qb - 1)
                    elif oc == 1:
                        norm_mult(qb - 1, 1)
                    elif oc == 2:
                        norm_mult(qb - 1, 2)
                    elif oc == 3:
                        norm_mult(qb - 1, 0)
                    elif oc == 4:
                        transposes(qb - 1, 0)
                    elif oc == 5:
                        transposes(qb - 1, 1)
            norm_sums(nqb - 1)
            for g in (1, 2, 0):
                norm_mult(nqb - 1, g)
            transposes(nqb - 1)
            proj_resid(kview(w_o[0]), lambda kc, t0, n: OT_v[:, kc, t0 - base:t0 - base + n],
                       lambda kc, t0, n: t_OT(kc, t0 - base, n), NCH, blocks, (lambda oc: dcol("cbo", oc)),
                       ln_cb=lambda t0, n, last: ln_stream(t0, n, "lmg1", "lmb1", "albm1", defer=last))

        plan(0)
        plan(1)
        for sbi in range(2):
            if sbi == 0:
                T = T0
                l0_blocks = [(0, 384), (384, 384), (768, 384)]
                l1_blocks = [(128, 512), (640, 512)]
                xs0, xn = 0, XC
                xdst0 = 0
                out_off = -128
            else:
                T = T1
                l0_blocks = [(0, 512), (512, 512)]
                l1_blocks = [(0, 512), (512, 512)]
                xs0, xn = XC, T1
                xdst0 = CH
                out_off = 1024
            if sbi == 0:
                pieces = [(0, CH + 384), (CH + 384, 384), (CH + 768, 384)]
                xT3 = xT.rearrange("(c p) t -> p c t", p=128)
                for pi, (c0, ncol) in enumerate(pieces):
                    P.op("pool", lambda e, c0=c0, ncol=ncol: e.dma_start(out=xb[:, :, c0:c0 + ncol], in_=xT3[:, :, c0:c0 + ncol]),
                         writes=[t for c in range(NCH) for t in tlx(c, c0, ncol)], dma_sem=f"xbp_{c0}")
                    if pi == 0:
                        wprefetch(2)
                P.op("pool", lambda e: e.dma_start(out=ident[:], in_=ident_d), writes=[t_ident], dma_sem="c3")
            def load_S(extra=None, T=T, xs0=xs0, xn=xn):
                for c in range(NCH):
                    P.op("sp", lambda e, c=c: e.dma_start(out=S[:, c, 0:T], in_=xT[c * 128:(c + 1) * 128, xs0 + xn - T:xs0 + xn]),
                         writes=tl(tS, c, 0, T), dma_sem=f"x{c}", extra=extra)
            flush()
            if debug_stage == "load0":
                break
            try:
                chk("load")
                conv_module(sbi, l0_blocks, T, load_S)
            except _Stop:
                break
            if debug_stage == "l0mix":
                break
            ffn(0, l0_blocks, "lfg0", "lfb0", "albf0", False, 0)
            if debug_stage == "l0":
                break
            q_reset()

            def q_early(l1_blocks=l1_blocks):
                q_prep()
                for oc in range(6):
                    q_unit(oc, 0, l1_blocks, l1_blocks[0][0], ahead=min(2, 5 - oc))
                    drip(2)
            kv_proj(sbi, l0_blocks, T, before_last=q_early)
            if debug_stage == "kv":
                break
            try:
                attention(sbi, l1_blocks)
            except _Stop:
                break
            if debug_stage == "l1mix":
                break
            def sb1_prefetch(t0, n):
                bi = (t0 - 128) // 512
                for c in range(NCH):
                    P.op("pool", lambda e, c=c, bi=bi: e.dma_start(out=xb[:, c, CH + bi * 512:CH + (bi + 1) * 512],
                                                                  in_=xT[c * 128:(c + 1) * 128, XC + bi * 512:XC + (bi + 1) * 512]),
                         writes=tlx(c, CH + bi * 512, 512), dma_sem=f"xn{c}_{bi}")
            ffn(1, l1_blocks, "lfg1", "lfb1", None, True, out_off, after_stats=(sb1_prefetch if sbi == 0 else None))

        flush()
        if debug_stage is not None:
            dbg = nc.dram_tensor("dbgS", [D, T0], F32, kind="ExternalOutput").ap()
            for c in range(NCH):
                P.op("sp", lambda e, c=c: e.dma_start(out=dbg[c * 128:(c + 1) * 128, :], in_=S[:, c, :]),
                     reads=tl(tS, c, 0, T0), dma_sem="dbg")
        P.emit()
        nc._prog_stats = {e: len(P.ops[e]) for e in P.ENGS}
        nc._pe_tags = [o.tag for o in P.ops["pe"]]
    return nc


def make_in_maps(inputs):
    inp = {k: np.asarray(v) for k, v in inputs.items()}
    x = inp["x"][0]
    xt_full = np.ascontiguousarray(x.T)
    pad = CH + HALO
    xt_pad = np.concatenate([np.zeros((D, pad), np.float32), xt_full], axis=1)
    ident = np.eye(128, dtype=np.float32)
    afac = alibi_factor()
    bv_bc = np.ascontiguousarray(np.broadcast_to(inp["kv_b_v"][None, :], (128, 128))).astype(np.float32)
    sinks_bc = np.ascontiguousarray(np.broadcast_to(inp["attn_sinks"][0][None, :], (128, NH))).astype(np.float32)
    shared = {
        "bv_bc": bv_bc, "sinks_bc": sinks_bc, "ident": ident, "afac": afac,
        "conv_w_pw1": inp["conv_w_pw1"], "conv_w_pw2": inp["conv_w_pw2"],
        "kv_w_k": inp["kv_w_k"], "kv_w_v": inp["kv_w_v"],
        "attn_w_q": inp["attn_w_q"], "attn_w_o": inp["attn_w_o"],
        "ffn_w_gate": inp["ffn_w_gate"], "ffn_w_up": inp["ffn_w_up"], "ffn_w_down": inp["ffn_w_down"],
    }
    maps = []
    for r in range(NCORES):
        m = dict(shared)
        m["xT"] = np.ascontiguousarray(xt_pad[:, r * TPC:r * TPC + XTOT])
        m["params"] = pack_params(inp, r)
        maps.append(m)
    return maps


_NC_CACHE = {}


def kernel(**inputs):
    if "nc" not in _NC_CACHE:
        _NC_CACHE["nc"] = build_nc()
    nc = _NC_CACHE["nc"]
    maps = make_in_maps(inputs)
    res = run_bass_kernel_spmd(nc, maps, core_ids=list(range(NCORES)))
    out = np.empty((1, SEQ, D), np.float32)
    for r in range(NCORES):
        out[0, r * TPC:(r + 1) * TPC, :] = res.results[r]["outT"].T
    return out
```

```python
import contextlib
import numpy as np
import concourse.bass as bass
import concourse.mybir as mybir
from concourse.bass_utils import run_bass_kernel_spmd

F32 = mybir.dt.float32
BF16 = mybir.dt.bfloat16
AF = mybir.ActivationFunctionType
ALU = mybir.AluOpType

D = 1024
NCH = 8
FF = 2816
NFC = 22
SEQ = 16384
NCORES = 8
TPC = 2048
HALO = 128
CW = 31
CH = 30
T0 = 1152
T1 = 1024
XC = CH + T0
XTOT = CH + HALO + TPC
ALPHA = float(2.0 ** 0.5)
EPS = 1e-5
NH = 16
NSLOT = 8
DEFER_LN = False
SLOTC = 2816


class Tile:
    __slots__ = ("w", "r")

    def __init__(self):
        self.w = None
        self.r = []


class Op:
    __slots__ = ("eng", "fn", "deps", "idx", "signal", "rank", "dma", "grp", "tag")


class Prog:
    ENGS = ("pe", "dve", "act", "pool", "sp")

    def __init__(self, nc):
        self.nc = nc
        self.ops = {e: [] for e in self.ENGS}
        self.dma_sems = {}
        self.pe_grp_last = []
        self.phase = ""

    def op(self, eng, fn, reads=(), writes=(), dma_sem=None, pe_acc=False, extra=None):
        deps = set()
        for t in reads:
            if t.w is not None:
                deps.add(t.w)
        for t in writes:
            if t.w is not None:
                deps.add(t.w)
            deps.update(t.r)
        if eng == "pe":
            deps = {d for d in deps if not (d[0] == "eng" and d[1] == "pe")}
        if extra:
            deps |= extra
        lst = self.ops[eng]
        o = Op()
        o.eng, o.fn, o.deps, o.idx, o.signal, o.rank, o.dma = eng, fn, deps, len(lst), False, 0, None
        if dma_sem is not None:
            self.dma_sems[dma_sem] = self.dma_sems.get(dma_sem, 0) + 16
            o.dma = (dma_sem, self.dma_sems[dma_sem])
            tok = ("dma", o.dma[0], o.dma[1])
        else:
            tok = ("eng", eng, o.idx)
        o.grp = None
        o.tag = self.phase
        if eng == "pe":
            if pe_acc and self.pe_grp_last:
                self.pe_grp_last[-1] = o.idx
            else:
                self.pe_grp_last.append(o.idx)
            o.grp = len(self.pe_grp_last) - 1
        lst.append(o)
        for t in writes:
            t.w = tok
            t.r = []
        for t in reads:
            if t.w is not tok:
                t.r.append(tok)
        return o

    def emit(self, final_wait_eng="sp"):
        nc = self.nc
        pe_ops = self.ops["pe"]
        for e in self.ENGS:
            for o in self.ops[e]:
                nd = set()
                for d in o.deps:
                    if d[0] == "eng" and d[1] == "pe":
                        d = ("eng", "pe", self.pe_grp_last[pe_ops[d[2]].grp])
                    nd.add(d)
                o.deps = nd
                for d in o.deps:
                    if d[0] == "eng":
                        self.ops[d[1]][d[2]].signal = True
        for e in self.ENGS:
            r = 0
            for o in self.ops[e]:
                if o.signal and o.dma is None:
                    r += 1
                o.rank = r
        with contextlib.ExitStack() as st:
            esem = {e: st.enter_context(nc.semaphore("s_" + e)) for e in self.ENGS}
            dsem = {n: st.enter_context(nc.semaphore("d_" + n)) for n in self.dma_sems}
            block = st.enter_context(nc.Block())
            hw = {"pe": block.tensor, "dve": block.vector, "act": block.scalar,
                  "pool": block.gpsimd, "sp": block.sync}

            def make(e):
                def body(eng):
                    waited = {}
                    for o in self.ops[e]:
                        need = {}
                        for d in o.deps:
                            if d[0] == "eng":
                                key = ("e", d[1])
                                v = self.ops[d[1]][d[2]].rank
                            else:
                                key = ("d", d[1])
                                v = d[2]
                            if v > need.get(key, 0):
                                need[key] = v
                        for key, v in need.items():
                            if waited.get(key, 0) >= v:
                                continue
                            waited[key] = v
                            eng.wait_ge(esem[key[1]] if key[0] == "e" else dsem[key[1]], v)
                        ins = o.fn(eng)
                        if o.dma is not None:
                            ins.then_inc(dsem[o.dma[0]], 16)
                        elif o.signal:
                            ins.then_inc(esem[e], 1)
                    if e == final_wait_eng:
                        for n, v in self.dma_sems.items():
                            if waited.get(("d", n), 0) < v:
                                eng.wait_ge(dsem[n], v)
                return body

            for e in self.ENGS:
                hw[e](make(e))


PCOLS = {}
_off = 0
for _n, _w in [("b_pw1", 16), ("w_dw", 8 * CW), ("b_dw", 8), ("cln_g", 8), ("cln_b", 8), ("b_pw2", 8),
               ("lmg0", 8), ("lmb0", 8), ("lfg0", 8), ("lfb0", 8), ("lmg1", 8), ("lmb1", 8),
               ("lfg1", 8), ("lfb1", 8), ("b_kd", 2), ("b_q", 8), ("b_o", 8), ("mask", 1)]:
    PCOLS[_n] = _off
    _off += _w
NPC = _off
DCOLS = {"albm0": 0, "albf0": 8, "albm1": 16, "bq8": 24, "agm0": 32, "agf0": 40, "agm1": 48, "cbo": 56}
NDC = 64


def _colvec(v):
    return np.ascontiguousarray(v.reshape(-1, 128).T)


def pack_params(inp, core):
    p = np.zeros((128, NPC), np.float32)

    def put(name, arr):
        p[:, PCOLS[name]:PCOLS[name] + arr.shape[1]] = arr
    put("b_pw1", _colvec(inp["conv_b_pw1"][0]))
    wd = inp["conv_w_dw"][0]
    put("w_dw", np.ascontiguousarray(wd.T.reshape(8, 128, CW).transpose(1, 0, 2).reshape(128, 8 * CW)))
    put("b_dw", _colvec(inp["conv_b_dw"][0]))
    put("cln_g", _colvec(inp["conv_ln_g"][0]))
    put("cln_b", _colvec(inp["conv_ln_b"][0]))
    put("b_pw2", _colvec(inp["conv_b_pw2"][0]))
    for l in range(2):
        put(f"lmg{l}", _colvec(inp["ln_mix_g"][l]))
        put(f"lmb{l}", _colvec(inp["ln_mix_b"][l]))
        put(f"lfg{l}", _colvec(inp["ln_ffn_g"][l]))
        put(f"lfb{l}", _colvec(inp["ln_ffn_b"][l]))
    bk = inp["kv_b_k"].reshape(2, 64)
    put("b_kd", np.ascontiguousarray(np.concatenate([bk, bk], axis=1).T))
    put("b_q", _colvec(inp["attn_b_q"][0]))
    put("b_o", _colvec(inp["attn_b_o"][0]))
    put("mask", np.full((128, 1), 0.0 if core == 0 else 1.0, np.float32))
    return p


def alibi_factor():
    i = np.arange(128)[None, :]
    j = np.arange(128)[:, None]
    slopes = np.exp2(-8.0 * np.arange(1, NH + 1, dtype=np.float64) / NH)
    A = np.zeros((128, NH, 2, 128), np.float64)
    d_prev = i + 128 - j
    d_cur = i - j
    for h in range(NH):
        A[:, h, 0, :] = np.where((d_prev >= 0) & (d_prev < 128), np.exp(-slopes[h] * d_prev), 0.0)
        A[:, h, 1, :] = np.where((d_cur >= 0) & (d_cur < 128), np.exp(-slopes[h] * d_cur), 0.0)
    return A.reshape(128, NH * 256).astype(np.float32)


class _Stop(Exception):
    pass


def build_nc(debug_stage=None):
    nc = bass.Bass("TRN2", target_bir_lowering=False)

    def chk(name):
        if debug_stage == name:
            raise _Stop()

    def din(name, shape):
        return nc.dram_tensor(name, list(shape), F32, kind="ExternalInput").ap()

    xT = din("xT", [D, XTOT])
    params_d = din("params", [128, NPC])
    bv_d = din("bv_bc", [128, 128])
    sink_d = din("sinks_bc", [128, NH])
    ident_d = din("ident", [128, 128])
    afac_d = din("afac", [128, NH * 256])
    w_pw1 = din("conv_w_pw1", [1, D, 2 * D])
    w_pw2 = din("conv_w_pw2", [1, D, D])
    w_k = din("kv_w_k", [D, 128])
    w_v = din("kv_w_v", [D, 128])
    w_q = din("attn_w_q", [1, D, D])
    w_o = din("attn_w_o", [1, D, D])
    w_g = din("ffn_w_gate", [2, D, FF])
    w_u = din("ffn_w_up", [2, D, FF])
    w_d = din("ffn_w_down", [2, FF, D])
    outT = nc.dram_tensor("outT", [D, TPC], F32, kind="ExternalOutput").ap()

    def kview(w2d):
        return w2d.rearrange("(kc p) n -> p kc n", p=128)

    with contextlib.ExitStack() as st:
        def sb(name, shape, dt):
            return st.enter_context(nc.sbuf_tensor(name, list(shape), dt))

        S = sb("S", [128, NCH, T0], F32)
        xb = sb("xb", [128, NCH, XC], BF16)
        zsq = sb("zsq", [128, NCH, T0], BF16)
        R = sb("R", [128, 25344], BF16)
        wring = [sb(f"wr{i}", [128, SLOTC], BF16) for i in range(NSLOT)]
        params = sb("params_sb", [128, NPC], F32)
        dpar = sb("dpar", [128, NDC], F32)
        ident = sb("ident_sb", [128, 128], BF16)
        onesD = sb("onesD", [128, 128], BF16)
        epsc = sb("epsc", [128, 1], F32)
        lndummy = sb("lndummy", [128, 1], F32)
        afac = sb("afac_sb", [128, NH * 256], BF16)
        bv_bc = sb("bv_sb", [128, 128], F32)
        esink = sb("esink", [128, NH + 2], F32)
        KT = sb("KT", [128, 2, T0], BF16)
        Vaug = sb("Vaug", [128, 9, 2, 65], BF16)
        htail = sb("htail", [128, NCH, CH], BF16)
        tmpf = [sb(f"tmpf{i}", [128, 512], F32) for i in range(2)]
        lnt = [None if i == 1 else sb(f"lnt{i}", [128, 512], F32) for i in range(5)]
        ebt = [sb(f"eb{i}", [128, 512], BF16) for i in range(2)]
        ptt = [sb(f"pt{i}", [128, 512], BF16) for i in range(4)]
        otok = [sb(f"otok{i}", [128, D + 128], BF16) for i in range(2)]
        rs_t = sb("rs_t", [128, 2, NH + 2], F32)
        ps = [st.enter_context(nc.psum_tensor(f"ps{i}", [128, 512], F32)) for i in range(3)]
        psO = st.enter_context(nc.psum_tensor("psO", [128, 3, 512], F32))
        ps += [psO[:, i, :] for i in range(3)]
        ps += [st.enter_context(nc.psum_tensor(f"ps{i}", [128, 512], F32)) for i in range(6, 8)]

        Rb = R[:]
        h_v = Rb[:, 0:NCH * XC].rearrange("p (c t) -> p c t", c=NCH)
        hs_v = Rb[:, NCH * XC:NCH * XC + NCH * T0].rearrange("p (c t) -> p c t", c=NCH)
        DG0 = NCH * XC + NCH * T0
        dg_v = [Rb[:, DG0:DG0 + CW * 128].rearrange("p (k m) -> p k m", k=CW),
                afac[:, 0:CW * 128].rearrange("p (k m) -> p k m", k=CW)]
        hid_v = Rb[:, 0:NFC * T0].rearrange("p (c t) -> p c t", c=NFC)
        qT_v = Rb[:, 0:NCH * T1].rearrange("p (c t) -> p c t", c=NCH)
        OT_v = Rb[:, NCH * T1:2 * NCH * T1].rearrange("p (c t) -> p c t", c=NCH)
        qTb_v = Rb[:, 2 * NCH * T1:3 * NCH * T1].rearrange("p (c t) -> p c t", c=NCH)

        P = Prog(nc)

        NT = T0 // 128
        tS = [[Tile() for _ in range(NT)] for _ in range(NCH)]
        tX = [[Tile() for _ in range(NT + 1)] for _ in range(NCH)]
        tQ = [[Tile() for _ in range(NT)] for _ in range(NCH)]
        tR = [[Tile() for _ in range(NT + 1)] for _ in range(NFC)]
        RT = 256
        nRt = 50688 // RT
        tRb = [Tile() for _ in range(nRt)]

        def rtiles(b0, b1):
            return tRb[b0 // RT:(b1 + RT - 1) // RT]

        def t_h(c, col0, ncol):
            b0 = (c * XC + col0) * 2
            return rtiles(b0, b0 + ncol * 2)

        def t_hs(c, t0, n):
            b0 = (NCH * XC + c * T0 + t0) * 2
            return rtiles(b0, b0 + n * 2)

        def t_dg(i):
            if i == 1:
                return [t_afac]
            b0 = (NCH * XC + NCH * T0) * 2
            return rtiles(b0, b0 + CW * 128 * 2) + [t_dgb[0]]

        def t_hid(c, t0, n):
            b0 = (c * T0 + t0) * 2
            return rtiles(b0, b0 + n * 2)

        def t_qT(c, t0, n):
            b0 = (c * T1 + t0) * 2
            return rtiles(b0, b0 + n * 2)

        def t_qTb(c, t0, n):
            b0 = (2 * NCH * T1 + c * T1 + t0) * 2
            return rtiles(b0, b0 + n * 2)

        def t_OT(c, t0, n):
            b0 = (NCH * T1 + c * T1 + t0) * 2
            return rtiles(b0, b0 + n * 2)

        def tl(arr, c, t0, n):
            return arr[c][t0 // 128:(t0 + n - 1) // 128 + 1]

        def tlx(c, col0, ncol):
            res = []
            if col0 < CH:
                res.append(tX[c][0])
            a = max(col0 - CH, 0)
            b = col0 + ncol - CH
            if b > a:
                res += tX[c][1 + a // 128:1 + (b - 1) // 128 + 1]
            return res

        t_par, t_dpar, t_ident, t_ones, t_afac, t_bv, t_esink, t_htail = [Tile() for _ in range(8)]
        t_dgb = [Tile(), Tile()]
        t_lndummy = Tile()
        t_dg2 = [Tile(), Tile()]
        t_KT = [[Tile() for _ in range(NT)] for _ in range(2)]
        t_V = [Tile() for _ in range(9)]
        t_Vones = Tile()
        t_wr = [[Tile(), Tile(), Tile(), Tile()] for _ in range(NSLOT)]
        t_tmpf = [Tile() for _ in range(2)]
        t_lnt = [Tile() for _ in range(5)]
        t_eb = [Tile() for _ in range(2)]
        t_pt = [Tile() for _ in range(4)]
        t_otok = [Tile() for _ in range(2)]
        t_otok3 = [[Tile() for _ in range(3)] for _ in range(2)]
        t_rs = [Tile() for _ in range(2)]
        t_ost = [Tile() for _ in range(3)]
        t_ps = [Tile() for _ in range(8)]

        cnt = {"bank": 0, "slot": 0, "tmpf": 0, "ost": 0, "eb": 0, "pt": 0, "ln": 0}

        def bank(lo=0, hi=8):
            b = lo + cnt["bank"] % (hi - lo)
            cnt["bank"] += 1
            return b

        def rot(name, n):
            v = cnt[name] % n
            cnt[name] += 1
            return v

        def pcol(name, c=0):
            o = PCOLS[name] + c
            return params[:, o:o + 1]

        def dcol(name, c=0):
            o = DCOLS[name] + c
            return dpar[:, o:o + 1]

        P.op("sp", lambda e: e.dma_start(out=params[:], in_=params_d), writes=[t_par], dma_sem="c0")
        P.op("sp", lambda e: e.dma_start(out=bv_bc[:], in_=bv_d), writes=[t_bv], dma_sem="c1")
        P.op("dve", lambda e: e.memset(esink[:], 0.0), writes=[t_esink])
        P.op("sp", lambda e: e.dma_start(out=esink[:, 0:NH], in_=sink_d), writes=[t_esink], dma_sem="c2")
        P.op("dve", lambda e: e.memset(onesD[:], 1.0 / D), writes=[t_ones])
        P.op("dve", lambda e: e.memset(epsc[:], EPS), writes=[t_ones])
        P.op("dve", lambda e: e.memset(Vaug[:, :, :, 64:65], 1.0), writes=[t_Vones])
        P.op("act", lambda e: e.activation(out=esink[:], in_=esink[:], func=AF.Exp), reads=[t_esink], writes=[t_esink])
        for nm, src, sc in [("albm0", "lmb0", ALPHA), ("albf0", "lfb0", ALPHA), ("albm1", "lmb1", ALPHA), ("bq8", "b_q", 0.125),
                            ("agm0", "lmg0", ALPHA), ("agf0", "lfg0", ALPHA), ("agm1", "lmg1", ALPHA)]:
            P.op("dve", lambda e, nm=nm, src=src, sc=sc: e.tensor_scalar(
                out=dpar[:, DCOLS[nm]:DCOLS[nm] + 8], in0=params[:, PCOLS[src]:PCOLS[src] + 8],
                scalar1=sc, scalar2=None, op0=ALU.mult), reads=[t_par], writes=[t_dpar])
        P.op("dve", lambda e: e.tensor_tensor(out=dpar[:, DCOLS["cbo"]:DCOLS["cbo"] + 8], in0=params[:, PCOLS["b_o"]:PCOLS["b_o"] + 8],
                                              in1=dpar[:, DCOLS["albf0"]:DCOLS["albf0"] + 8], op=ALU.add),
             reads=[t_par, t_dpar], writes=[t_dpar])

        WL = []
        wstate = {"issued": 0, "next": 0, "held": []}
        AHEAD = 2

        def plan(sbi_):
            def std(c0, kc, src, ti):
                return (ti, c0, kc, 128, 0, 128, src)
            wp1 = kview(w_pw1[0]); wp2 = kview(w_pw2[0]); wqv = kview(w_q[0]); wov = kview(w_o[0])
            for oc in range(NCH):
                WL.append([std(0, 8, wp1[:, :, oc * 128:(oc + 1) * 128], 0),
                           std(1024, 8, wp1[:, :, D + oc * 128:D + (oc + 1) * 128], 1)])
            for oc in range(NCH):
                WL.append([std(0, 8, wp2[:, :, oc * 128:(oc + 1) * 128], 0)])

            def ffn_plan(layer):
                wg = kview(w_g[layer]); wu = kview(w_u[layer]); wd = kview(w_d[layer])
                for fc in range(NFC):
                    WL.append([std(0, 8, wg[:, :, fc * 128:(fc + 1) * 128], 0),
                               std(1024, 8, wu[:, :, fc * 128:(fc + 1) * 128], 1)])
                for oc in range(NCH):
                    WL.append([std(0, 11, wd[:, 0:11, oc * 128:(oc + 1) * 128], 0),
                               std(1408, 11, wd[:, 11:22, oc * 128:(oc + 1) * 128], 1)])
            ffn_plan(0)
            wkv = kview(w_k); wvv = kview(w_v)
            kvp = []
            for kvh in range(2):
                for dup in range(2):
                    kvp.append((2 * kvh + dup, kvh * 1024, 8, 128, dup * 64, dup * 64 + 64, wkv[:, :, kvh * 64:(kvh + 1) * 64]))
            WL.append(kvp)
            WL.append([std(0, 8, wvv, 0)])
            for oc in range(NCH):
                WL.append([std(0, 8, wqv[:, :, oc * 128:(oc + 1) * 128], 0)])
            for oc in range(NCH):
                WL.append([std(0, 8, wov[:, :, oc * 128:(oc + 1) * 128], 0)])
            ffn_plan(1)

        def whold():
            wstate["held"].append(wstate["next"])

        def wrelease():
            wstate["held"].pop(0)

        def _issue(i):
            assert not wstate["held"] or i - NSLOT < min(wstate["held"]), ("weight slot reuse before consumers recorded", i, wstate["held"])
            s = i % NSLOT
            extra = set()
            for t in t_wr[s]:
                extra.update(t.r)
                t.r = []
            for (ti, c0, kc, n, lo, hi, src_ap) in WL[i]:
                dst = wring[s][:, c0:c0 + kc * n].rearrange("p (k n) -> p k n", k=kc)[:, :, lo:hi]

                P.op("pool", lambda e, dst=dst, src_ap=src_ap: e.dma_start(out=dst, in_=src_ap),
                     writes=[t_wr[s][ti]], dma_sem=f"w{s}_{ti}", extra=set(extra))

        def wnext(ahead=AHEAD):
            i = wstate["next"]
            wstate["next"] += 1
            while wstate["issued"] < min(len(WL), i + 1 + ahead):
                _issue(wstate["issued"])
                wstate["issued"] += 1
            s = i % NSLOT
            tis = sorted({p[0] for p in WL[i]})
            return s, [t_wr[s][ti] for ti in tis]

        def wprefetch(k):
            while wstate["issued"] < min(len(WL), wstate["next"] + k):
                _issue(wstate["issued"])
                wstate["issued"] += 1

        def wv(s, c0, kc, n=128):
            return wring[s][:, c0:c0 + kc * n].rearrange("p (k n) -> p k n", k=kc)

        def ln_stats(zb_ap, zb_tiles, sq_ap, sq_tiles, n):
            pm = bank()
            pq = bank()
            for kc in range(NCH):
                P.op("pe", lambda e, kc=kc, pm=pm: e.matmul(ps[pm][:, 0:n], lhsT=onesD[:], rhs=zb_ap[:, kc, :],
                                                            start=(kc == 0), stop=(kc == NCH - 1)),
                     reads=[t_ones] + zb_tiles[kc], writes=[t_ps[pm]], pe_acc=(kc > 0))
            for kc in range(NCH):
                P.op("pe", lambda e, kc=kc, pq=pq: e.matmul(ps[pq][:, 0:n], lhsT=onesD[:], rhs=sq_ap[:, kc, :],
                                                            start=(kc == 0), stop=(kc == NCH - 1)),
                     reads=[t_ones] + sq_tiles[kc], writes=[t_ps[pq]], pe_acc=(kc > 0))
            tfi = rot("tmpf", 2)
            par = rot("ln", 2)
            mi, ri = (0, 2) if par == 0 else (3, 4)
            P.op("act", lambda e: e.activation(out=lndummy[:, 0:1], in_=epsc[:, 0:1], func=AF.Ln), reads=[t_ones], writes=[t_lndummy])
            P.op("dve", lambda e: e.tensor_copy(out=lnt[mi][:, 0:n], in_=ps[pm][:, 0:n]),
                 reads=[t_ps[pm]], writes=[t_lnt[mi]])
            P.op("dve", lambda e: e.tensor_tensor(out=tmpf[tfi][:, 0:n], in0=lnt[mi][:, 0:n], in1=lnt[mi][:, 0:n], op=ALU.mult),
                 reads=[t_lnt[mi]], writes=[t_tmpf[tfi]])
            P.op("dve", lambda e: e.tensor_tensor(out=tmpf[tfi][:, 0:n], in0=ps[pq][:, 0:n], in1=tmpf[tfi][:, 0:n], op=ALU.subtract),
                 reads=[t_ps[pq], t_tmpf[tfi]], writes=[t_tmpf[tfi]])
            P.op("act", lambda e: e.activation(out=tmpf[tfi][:, 0:n], in_=tmpf[tfi][:, 0:n], func=AF.Ln, bias=epsc[:, 0:1], scale=1.0),
                 reads=[t_tmpf[tfi], t_ones], writes=[t_tmpf[tfi]])
            P.op("act", lambda e: e.activation(out=lnt[ri][:, 0:n], in_=tmpf[tfi][:, 0:n], func=AF.Exp, scale=-0.5),
                 reads=[t_tmpf[tfi]], writes=[t_lnt[ri]])
            return mi, ri

        AGN = {"lmg0": "agm0", "lfg0": "agf0", "lmg1": "agm1"}
        pending = []

        def drip(k=1):
            for _ in range(min(k, len(pending))):
                pending.pop(0)()

        def flush():
            drip(len(pending))

        def ln_stream(t0, n, gname, bname, albname, final=False, out_t0=None, defer=False):
            def body():
                P.phase = f"lnstats{t0}"
                zb_ap = xb[:, :, CH + t0:CH + t0 + n]
                sq_ap = zsq[:, :, t0:t0 + n]
                mi, ri = ln_stats(zb_ap, [tlx(c, CH + t0, n) for c in range(NCH)], sq_ap, [tl(tQ, c, t0, n) for c in range(NCH)], n)
                for c in range(NCH):
                    sv = S[:, c, t0:t0 + n]
                    st_ = tl(tS, c, t0, n)
                    P.op("pool" if c % 3 == 0 else "dve", lambda e, sv=sv: e.tensor_tensor(out=sv, in0=sv, in1=lnt[mi][:, 0:n], op=ALU.subtract),
                         reads=st_ + [t_lnt[mi]], writes=st_)
                    gcol = pcol(gname, c) if final else dcol(AGN[gname], c)
                    P.op("dve", lambda e, sv=sv, gcol=gcol: e.scalar_tensor_tensor(out=sv, in0=sv, scalar=gcol, in1=lnt[ri][:, 0:n],
                                                                               op0=ALU.mult, op1=ALU.mult),
                         reads=st_ + [t_lnt[ri], t_par, t_dpar], writes=st_)
                    if final:
                        P.op("act", lambda e, sv=sv, c=c: e.activation(out=sv, in_=sv, func=AF.Identity, bias=pcol(bname, c), scale=1.0),
                             reads=st_ + [t_par], writes=st_)
                        P.op("sp", lambda e, sv=sv, c=c: e.dma_start(out=outT[c * 128:(c + 1) * 128, out_t0:out_t0 + n], in_=sv),
                             reads=st_, dma_sem=f"o{c}")
                    else:
                        P.op("act", lambda e, sv=sv, c=c: e.activation(out=xb[:, c, CH + t0:CH + t0 + n], in_=sv, func=AF.Identity,
                                                                       bias=pcol(bname, c), scale=1.0 / ALPHA),
                             reads=st_ + [t_par], writes=tlx(c, CH + t0, n))

            if defer and DEFER_LN:
                pending.append(body)
            else:
                flush()
                body()

        def resid_evac(pb, c, t0, n, bias_ap):
            sv = S[:, c, t0:t0 + n]
            st_ = tl(tS, c, t0, n)
            if bias_ap is not None:
                P.op("dve", lambda e: e.scalar_tensor_tensor(out=sv, in0=ps[pb][:, 0:n], scalar=bias_ap, in1=sv,
                                                             op0=ALU.add, op1=ALU.add),
                     reads=[t_ps[pb], t_par, t_dpar] + st_, writes=st_)
            else:
                P.op("dve", lambda e: e.tensor_tensor(out=sv, in0=ps[pb][:, 0:n], in1=sv, op=ALU.add),
                     reads=[t_ps[pb]] + st_, writes=st_)
            P.op("act", lambda e: e.activation(out=xb[:, c, CH + t0:CH + t0 + n], in_=sv, func=AF.Copy),
                 reads=st_, writes=tlx(c, CH + t0, n))
            P.op("pool", lambda e: e.tensor_tensor(out=zsq[:, c, t0:t0 + n], in0=sv, in1=sv, op=ALU.mult),
                 reads=st_, writes=tl(tQ, c, t0, n))

        def proj_resid(wsrc, rhs_ap_fn, rhs_tiles_fn, nk, blocks, bias_name, ln_cb=None):
            P.phase = f"resid_nk{nk}"
            flush()
            ws = []
            whold()
            first = wstate["next"]
            for oc in range(NCH):
                s, wt = wnext(ahead=NCH - 1 - oc + (NSLOT - NCH))
                ws.append((wv(s, 0, nk), wt))
            prev = None
            for (t0, n) in blocks:
                for oc in range(NCH):
                    w, wt = ws[oc]
                    pb = bank()
                    for kc in range(nk):
                        P.op("pe", lambda e, kc=kc, pb=pb, t0=t0, n=n, w=w: e.matmul(ps[pb][:, 0:n], lhsT=w[:, kc, :], rhs=rhs_ap_fn(kc, t0, n),
                                                                                 start=(kc == 0), stop=(kc == nk - 1)),
                             reads=wt + rhs_tiles_fn(kc, t0, n), writes=[t_ps[pb]], pe_acc=(kc > 0))
                    resid_evac(pb, oc, t0, n, bias_name(oc) if callable(bias_name) else (pcol(bias_name, oc) if bias_name else None))
                    if oc == 0 and prev is not None:
                        if ln_cb is not None:
                            ln_cb(prev[0], prev[1], False)
                        prev = None
                    if (t0, n) == blocks[-1]:
                        wstate["held"][0] = first + oc + 1
                        if oc < AHEAD:
                            wprefetch(oc + 1)
                if (t0, n) == blocks[-1]:
                    wrelease()
                    wprefetch(AHEAD)
                    if ln_cb is not None:
                        ln_cb(t0, n, True)
                else:
                    prev = (t0, n)

        def ffn(layer, blocks, gname, bname, albname, final, out_off, after_stats=None):
            wg = kview(w_g[layer])
            wu = kview(w_u[layer])
            wd = kview(w_d[layer])
            def gu(fc, wgv, wuv, wt, t0, n):
                pg = bank()
                pu = bank()
                for kc in range(NCH):
                    P.op("pe", lambda e, kc=kc: e.matmul(ps[pg][:, 0:n], lhsT=wgv[:, kc, :], rhs=xb[:, kc, CH + t0:CH + t0 + n],
                                                         start=(kc == 0), stop=(kc == NCH - 1)),
                         reads=[wt[0]] + tlx(kc, CH + t0, n), writes=[t_ps[pg]], pe_acc=(kc > 0))
                for kc in range(NCH):
                    P.op("pe", lambda e, kc=kc: e.matmul(ps[pu][:, 0:n], lhsT=wuv[:, kc, :], rhs=xb[:, kc, CH + t0:CH + t0 + n],
                                                         start=(kc == 0), stop=(kc == NCH - 1)),
                         reads=[wt[1]] + tlx(kc, CH + t0, n), writes=[t_ps[pu]], pe_acc=(kc > 0))
                tf = rot("tmpf", 2)
                P.op("act", lambda e: e.activation(out=tmpf[tf][:, 0:n], in_=ps[pg][:, 0:n], func=AF.Silu),
                     reads=[t_ps[pg]], writes=[t_tmpf[tf]])
                P.op("dve", lambda e: e.tensor_tensor(out=hid_v[:, fc, t0:t0 + n], in0=tmpf[tf][:, 0:n], in1=ps[pu][:, 0:n], op=ALU.mult),
                     reads=[t_ps[pu], t_tmpf[tf]], writes=t_hid(fc, t0, n))

            P.phase = f"ffn_gu"
            NP = 3 if len(blocks) >= 3 else 5
            pro = []
            whold()
            for fc in range(NP):
                s, wt = wnext()
                pro.append((fc, wv(s, 0, 8), wv(s, 1024, 8), wt))
            for (fc, wgv, wuv, wt) in pro:
                for (t0, n) in blocks[:-1]:
                    gu(fc, wgv, wuv, wt, t0, n)
                    drip(3)
            flush()
            for (fc, wgv, wuv, wt) in pro:
                gu(fc, wgv, wuv, wt, *blocks[-1])
            wrelease()
            for fc in range(NP, NFC):
                s, wt = wnext()
                wgv = wv(s, 0, 8)
                wuv = wv(s, 1024, 8)
                for (t0, n) in blocks:
                    gu(fc, wgv, wuv, wt, t0, n)
            def cb(t0, n, last):
                ln_stream(t0, n, gname, bname, albname, final=final, out_t0=(t0 + out_off) if final else None, defer=(last and not final))
                if after_stats is not None:
                    after_stats(t0, n)
            proj_resid(wd, lambda kc, t0, n: hid_v[:, kc, t0:t0 + n], lambda kc, t0, n: t_hid(kc, t0, n), NFC, blocks,
                       (lambda oc: dcol("albm0" if layer == 0 else "albm1", oc)), ln_cb=cb)

        def conv_module(sbi, blocks, T, load_S):
            P.phase = f"pw1_sb{sbi}"
            wp1 = kview(w_pw1[0])
            def build_diag(c, di):
                NDV = CW
                ex = set()
                for t in t_dg(di) + [t_dg2[di]]:
                    ex.update(t.r)
                    if t.w is not None:
                        ex.add(t.w)
                for k in range(CW):
                    wcol = params[:, PCOLS["w_dw"] + c * CW + k:PCOLS["w_dw"] + c * CW + k + 1]
                    if k < NDV:
                        P.op("dve", lambda e, di=di, k=k, wcol=wcol: e.tensor_scalar(out=dg_v[di][:, k, :], in0=ident[:], scalar1=wcol,
                                                                                 scalar2=None, op0=ALU.mult),
                             reads=[t_ident, t_par], writes=(t_dg(di) if k == NDV - 1 else []), extra=(set(ex) if k == 0 else None))
                    else:
                        P.op("act", lambda e, di=di, k=k, wcol=wcol: e.activation(out=dg_v[di][:, k, :], in_=ident[:], func=AF.Copy, scale=wcol),
                             reads=[t_ident, t_par], writes=([t_dg2[di]] if k == CW - 1 else []), extra=(set(ex) if k == NDV else None))

            P.phase = f"conv_sb{sbi}"
            if sbi == 1:
                P.op("pool", lambda e: e.tensor_copy(out=h_v[:, :, 0:CH], in_=htail[:]), reads=[t_htail],
                     writes=[t for c in range(NCH) for t in t_h(c, 0, CH)])
            pblocks = list(blocks)
            if sbi == 0:
                pblocks = [(-CH, CH + blocks[0][1])] + pblocks[1:]
            for oc in range(NCH):
                if oc == 3:
                    build_diag(0, 0)
                s, wt = wnext()
                wa = wv(s, 0, 8)
                wgt = wv(s, 1024, 8)
                for (t0, n) in pblocks:
                    pa = bank()
                    pg = bank()
                    for kc in range(NCH):
                        P.op("pe", lambda e, kc=kc, pa=pa, t0=t0, n=n, wa=wa: e.matmul(ps[pa][:, 0:n], lhsT=wa[:, kc, :],
                                                                                    rhs=xb[:, kc, CH + t0:CH + t0 + n],
                                                                                    start=(kc == 0), stop=(kc == NCH - 1)),
                             reads=[wt[0]] + tlx(kc, CH + t0, n), writes=[t_ps[pa]], pe_acc=(kc > 0))
                    for kc in range(NCH):
                        P.op("pe", lambda e, kc=kc, pg=pg, t0=t0, n=n, wgt=wgt: e.matmul(ps[pg][:, 0:n], lhsT=wgt[:, kc, :],
                                                                                      rhs=xb[:, kc, CH + t0:CH + t0 + n],
                                                                                      start=(kc == 0), stop=(kc == NCH - 1)),
                             reads=[wt[1]] + tlx(kc, CH + t0, n), writes=[t_ps[pg]], pe_acc=(kc > 0))
                    tf = rot("tmpf", 2)
                    P.op("act", lambda e, pg=pg, tf=tf, n=n, oc=oc: e.activation(out=tmpf[tf][:, 0:n], in_=ps[pg][:, 0:n], func=AF.Sigmoid,
                                                                              bias=pcol("b_pw1", 8 + oc), scale=1.0),
                         reads=[t_ps[pg], t_par], writes=[t_tmpf[tf]])
                    hdst = h_v[:, oc, CH + t0:CH + t0 + n]
                    glu_op = P.op("dve", lambda e, pa=pa, tf=tf, n=n, oc=oc, hdst=hdst: e.scalar_tensor_tensor(
                        out=hdst, in0=ps[pa][:, 0:n], scalar=pcol("b_pw1", oc), in1=tmpf[tf][:, 0:n], op0=ALU.add, op1=ALU.mult),
                        reads=[t_ps[pa], t_tmpf[tf], t_par], writes=t_h(oc, CH + t0, n))
                    if oc == 0 and (t0, n) == pblocks[0]:
                        load_S(extra={("eng", "dve", glu_op.idx)})
                    if t0 < 0:
                        hhalo = h_v[:, oc, 0:CH]
                        P.op("dve", lambda e, hhalo=hhalo: e.tensor_scalar(out=hhalo, in0=hhalo, scalar1=pcol("mask"), scalar2=None, op0=ALU.mult),
                             reads=t_h(oc, 0, CH) + [t_par], writes=t_h(oc, 0, CH))
                    drip(1)
            if sbi == 0:
                P.op("pool", lambda e: e.tensor_copy(out=htail[:], in_=h_v[:, :, T:T + CH]),
                     reads=[t for c in range(NCH) for t in t_h(c, T, CH)], writes=[t_htail])
            chk("pw1")
            flush()
            for c in range(NCH):
                for (t0, n) in blocks:
                    P.op("act", lambda e, c=c, t0=t0, n=n: e.activation(out=S[:, c, t0:t0 + n], in_=S[:, c, t0:t0 + n], func=AF.Copy, scale=ALPHA),
                         reads=tl(tS, c, t0, n), writes=tl(tS, c, t0, n))
            def conv_ln(t0, n):
                zb_ap = xb[:, :, CH + t0:CH + t0 + n]
                mi, ri = ln_stats(zb_ap, [tlx(c, CH + t0, n) for c in range(NCH)], zsq[:, :, t0:t0 + n],
                         [tl(tQ, c, t0, n) for c in range(NCH)], n)
                P.op("act", lambda e: e.activation(out=lndummy[:, 0:1], in_=epsc[:, 0:1], func=AF.Silu), reads=[t_ones], writes=[t_lndummy])
                for c in range(NCH):
                    tf = rot("tmpf", 2)
                    P.op("pool" if c % 4 == 3 else "dve", lambda e, c=c, tf=tf, t0=t0, n=n, mi=mi: e.tensor_tensor(out=tmpf[tf][:, 0:n], in0=xb[:, c, CH + t0:CH + t0 + n],
                                                                                 in1=lnt[mi][:, 0:n], op=ALU.subtract),
                         reads=tlx(c, CH + t0, n) + [t_lnt[mi]], writes=[t_tmpf[tf]])
                    P.op("dve", lambda e, c=c, tf=tf, n=n, ri=ri: e.scalar_tensor_tensor(out=tmpf[tf][:, 0:n], in0=tmpf[tf][:, 0:n], scalar=pcol("cln_g", c),
                                                                                in1=lnt[ri][:, 0:n], op0=ALU.mult, op1=ALU.mult),
                         reads=[t_tmpf[tf], t_lnt[ri], t_par], writes=[t_tmpf[tf]])
                    P.op("act", lambda e, c=c, tf=tf, t0=t0, n=n: e.activation(out=hs_v[:, c, t0:t0 + n], in_=tmpf[tf][:, 0:n], func=AF.Silu,
                                                                             bias=pcol("cln_b", c), scale=1.0),
                         reads=[t_tmpf[tf], t_par], writes=t_hs(c, t0, n))

            wprefetch(NCH)
            groups = [list(blocks)]
            dcount = [0]

            def conv_block(c, di, t0, n):
                pc = bank()
                for k in range(CW):
                    P.op("pe", lambda e, k=k: e.matmul(ps[pc][:, 0:n], lhsT=dg_v[di][:, k, :], rhs=h_v[:, c, t0 + k:t0 + k + n],
                                                       start=(k == 0), stop=(k == CW - 1)),
                         reads=((t_dg(di) + [t_dg2[di]]) if k in (0, CW - 1) else []) + t_h(c, t0 + k, n), writes=[t_ps[pc]], pe_acc=(k > 0))
                P.op("act", lambda e: e.activation(out=xb[:, c, CH + t0:CH + t0 + n], in_=ps[pc][:, 0:n], func=AF.Identity,
                                                   bias=pcol("b_dw", c), scale=1.0),
                     reads=[t_ps[pc], t_par], writes=tlx(c, CH + t0, n))
                P.op("dve", lambda e: e.scalar_tensor_tensor(out=zsq[:, c, t0:t0 + n], in0=ps[pc][:, 0:n], scalar=pcol("b_dw", c),
                                                             in1=xb[:, c, CH + t0:CH + t0 + n], op0=ALU.add, op1=ALU.mult),
                     reads=[t_ps[pc], t_par] + tlx(c, CH + t0, n), writes=tl(tQ, c, t0, n))

            for gi, grp in enumerate(groups):
                if gi > 0:
                    build_diag(0, dcount[0] % 2)
                dis = {}
                for c in range(NCH - 2):
                    dis[c] = dcount[0] % 2
                    dcount[0] += 1
                    build_diag(c + 1, dcount[0] % 2)
                    for bi, (t0, n) in enumerate(grp):
                        conv_block(c, dis[c], t0, n)
                c6, c7 = NCH - 2, NCH - 1
                dis[c6] = dcount[0] % 2
                dcount[0] += 1
                build_diag(c7, dcount[0] % 2)
                dis[c7] = dcount[0] % 2
                dcount[0] += 1
                for bi, (t0, n) in enumerate(grp):
                    conv_block(c6, dis[c6], t0, n)
                    conv_block(c7, dis[c7], t0, n)
                    conv_ln(t0, n)
            P.op("pool", lambda e: e.dma_start(out=afac[:], in_=afac_d), writes=[t_afac], dma_sem="c4")
            chk("conv")
            chk("cln")
            proj_resid(kview(w_pw2[0]), lambda kc, t0, n: hs_v[:, kc, t0:t0 + n], lambda kc, t0, n: t_hs(kc, t0, n), NCH, blocks, "b_pw2",
                       ln_cb=lambda t0, n, last: ln_stream(t0, n, "lmg0", "lmb0", "albm0", defer=last))

        qs = {"prep": False, "done": set(), "w": {}}

        def q_reset():
            qs["prep"] = False
            qs["done"] = set()
            qs["w"] = {}

        def q_prep():
            P.op("dve", lambda e: e.memset(qT_v[64:128, :, :], 0.0), writes=[t for c in range(NCH) for t in t_qT(c, 0, T1)])
            P.op("dve", lambda e: e.memset(qTb_v[0:64, :, :], 0.0), writes=[t for c in range(NCH) for t in t_qTb(c, 0, T1)])
            qs["prep"] = True
            whold()

        def q_unit(oc, bi, blocks, base, ahead=None):
            P.phase = f"qproj"
            if oc not in qs["w"]:
                s, wt = wnext() if ahead is None else wnext(ahead=ahead)
                qs["w"][oc] = (wv(s, 0, 8), wt)
            w, wt = qs["w"][oc]
            t0, n = blocks[bi]
            pb = bank()
            for kc in range(NCH):
                P.op("pe", lambda e, kc=kc: e.matmul(ps[pb][:, 0:n], lhsT=w[:, kc, :], rhs=xb[:, kc, CH + t0:CH + t0 + n],
                                                     start=(kc == 0), stop=(kc == NCH - 1)),
                     reads=wt + tlx(kc, CH + t0, n), writes=[t_ps[pb]], pe_acc=(kc > 0))
            P.op("act", lambda e: e.activation(out=qT_v[0:64, oc, t0 - base:t0 - base + n], in_=ps[pb][0:64, 0:n],
                                               func=AF.Identity, bias=dcol("bq8", oc)[0:64, :], scale=0.125),
                 reads=[t_ps[pb], t_dpar], writes=t_qT(oc, t0 - base, n))
            P.op("act", lambda e: e.activation(out=qTb_v[64:128, oc, t0 - base:t0 - base + n], in_=ps[pb][64:128, 0:n],
                                               func=AF.Identity, bias=dcol("bq8", oc)[64:128, :], scale=0.125),
                 reads=[t_ps[pb], t_dpar], writes=t_qTb(oc, t0 - base, n))
            qs["done"].add((oc, bi))

        def kv_proj(sbi, blocks, T, before_last=None):
            koff = 0 if sbi == 0 else 128
            if sbi == 1:
                for kvh in range(2):
                    P.op("pool", lambda e, kvh=kvh: e.tensor_copy(out=KT[:, kvh, 0:128], in_=KT[:, kvh, 1024:1152]),
                         reads=[t_KT[kvh][8]], writes=[t_KT[kvh][0]])
                P.op("pool", lambda e: e.tensor_copy(out=Vaug[:, 0, :, 0:64], in_=Vaug[:, 8, :, 0:64]), reads=[t_V[8]], writes=[t_V[0]])
            P.phase = f"kv_sb{sbi}"
            whold()
            s, wt_k = wnext()
            s2, wt_v = wnext()
            wt_kv = [wt_k, wt_v[0]]
            wprefetch(4)
            wk_ = [wv(s, 0, 8), wv(s, 1024, 8)]
            wv_ = wv(s2, 0, 8)
            for (t0, n) in blocks:
                if (t0, n) == blocks[-1]:
                    if before_last is not None:
                        before_last()
                    flush()
                for kvh in range(2):
                    pb = bank()
                    for kc in range(NCH):
                        P.op("pe", lambda e, kc=kc, pb=pb, t0=t0, n=n, kvh=kvh: e.matmul(ps[pb][:, 0:n], lhsT=wk_[kvh][:, kc, :],
                                                                                      rhs=xb[:, kc, CH + t0:CH + t0 + n],
                                                                                      start=(kc == 0), stop=(kc == NCH - 1)),
                             reads=wt_kv[0][2 * kvh:2 * kvh + 2] + tlx(kc, CH + t0, n), writes=[t_ps[pb]], pe_acc=(kc > 0))
                    P.op("act", lambda e, pb=pb, t0=t0, n=n, kvh=kvh: e.activation(out=KT[:, kvh, koff + t0:koff + t0 + n], in_=ps[pb][:, 0:n],
                                                                                 func=AF.Identity, bias=pcol("b_kd", kvh), scale=1.0),
                         reads=[t_ps[pb], t_par], writes=t_KT[kvh][(koff + t0) // 128:(koff + t0 + n - 1) // 128 + 1])
                for j in range(n // 128):
                    tt = t0 + j * 128
                    vt = (koff + tt) // 128
                    pb = bank()
                    for kc in range(NCH):
                        P.op("pe", lambda e, kc=kc, pb=pb, tt=tt: e.matmul(ps[pb][:, 0:128], lhsT=xb[:, kc, CH + tt:CH + tt + 128],
                                                                         rhs=wv_[:, kc, :], start=(kc == 0), stop=(kc == NCH - 1)),
                             reads=[wt_kv[1]] + tlx(kc, CH + tt, 128), writes=[t_ps[pb]], pe_acc=(kc > 0))
                    P.op("dve", lambda e, pb=pb, vt=vt: e.tensor_tensor(out=Vaug[:, vt, :, 0:64],
                                                                      in0=ps[pb][:, 0:128].rearrange("p (h d) -> p h d", h=2),
                                                                      in1=bv_bc[:].rearrange("p (h d) -> p h d", h=2), op=ALU.add),
                         reads=[t_ps[pb], t_bv], writes=[t_V[vt]])
                if (t0, n) != blocks[-1]:
                    drip(3)
            wrelease()

        def attention(sbi, blocks):
            koff = 0 if sbi == 0 else 128
            base = blocks[0][0]
            wqv = kview(w_q[0])
            if not qs["prep"]:
                q_prep()
            for oc in range(NCH):
                if oc == 4:
                    wrelease()
                for bi in range(len(blocks)):
                    if (oc, bi) not in qs["done"]:
                        q_unit(oc, bi, blocks, base)
            P.phase = f"attcore"
            nqb = T1 // 128
            pend = None

            def qk(qb, oc):
                kvh = oc // 4
                q0 = qb * 128
                kc_cur = koff + base + q0
                pS = bank(0, 3)
                for hh in range(2):
                    for part in range(2):
                        kc0 = kc_cur - 128 + part * 128
                        P.op("pe", lambda e, hh=hh, part=part, kc0=kc0, pS=pS, kvh=kvh, q0=q0, oc=oc: e.matmul(
                            ps[pS][:, (hh * 2 + part) * 128:(hh * 2 + part + 1) * 128],
                            lhsT=KT[:, kvh, kc0:kc0 + 128],
                            rhs=(qT_v if hh == 0 else qTb_v)[:, oc, q0:q0 + 128], start=True, stop=True),
                            reads=[t_KT[kvh][kc0 // 128]] + (t_qT(oc, q0, 128) if hh == 0 else t_qTb(oc, q0, 128)), writes=[t_ps[pS]], pe_acc=(hh + part > 0))
                eb = rot("eb", 2)
                pt = rot("pt", 4)
                P.op("act", lambda e, pS=pS, eb=eb: e.activation(out=ebt[eb][:], in_=ps[pS][:], func=AF.Exp),
                     reads=[t_ps[pS]], writes=[t_eb[eb]])
                P.op("dve", lambda e, eb=eb, pt=pt, oc=oc: e.tensor_tensor(out=ptt[pt][:], in0=ebt[eb][:],
                                                                       in1=afac[:, oc * 512:(oc + 1) * 512], op=ALU.mult),
                     reads=[t_eb[eb], t_afac], writes=[t_pt[pt]])
                if sbi == 0 and qb == 0:
                    v = ptt[pt][:].rearrange("p (h a i) -> p h a i", h=2, a=2)[:, :, 0, :]
                    P.op("dve", lambda e, v=v: e.tensor_scalar(out=v, in0=v, scalar1=pcol("mask"), scalar2=None, op0=ALU.mult),
                         reads=[t_pt[pt], t_par], writes=[t_pt[pt]])
                return pt

            def obank(qb, g):
                if g == 0:
                    return 3 if qb % 2 == 0 else 7
                return 3 + g

            def pv(qb, oc, pt):
                kvh = oc // 4
                vt_cur = (koff + base + qb * 128) // 128
                for hh in range(2):
                    hgl = 2 * oc + hh
                    pb = obank(qb, hgl // 6)
                    c0 = (hgl % 6) * 65
                    for part in range(2):
                        vt = vt_cur - 1 + part
                        P.op("pe", lambda e, hh=hh, part=part, vt=vt, pb=pb, c0=c0, pt=pt, kvh=kvh: e.matmul(
                            ps[pb][:, c0:c0 + 65], lhsT=ptt[pt][:, (hh * 2 + part) * 128:(hh * 2 + part + 1) * 128],
                            rhs=Vaug[:, vt, kvh, :], start=(part == 0), stop=(part == 1)),
                            reads=[t_pt[pt], t_V[vt], t_Vones], writes=[t_ps[pb]], pe_acc=(part > 0))

            GRP = [(0, 6), (6, 12), (12, 16)]

            def norm_sums(qb):
                ob = qb % 2
                for g, (h0, h1) in enumerate(GRP):
                    pbk = obank(qb, g)
                    pv3 = ps[pbk][:, 0:(h1 - h0) * 65].rearrange("p (h d) -> p h d", d=65)
                    P.op("dve", lambda e, pv3=pv3, h0=h0, h1=h1: e.tensor_tensor(
                        out=rs_t[:, ob, h0:h1].unsqueeze(2), in0=pv3[:, :, 64:65], in1=esink[:, h0:h1].unsqueeze(2), op=ALU.add),
                        reads=[t_ps[pbk], t_esink], writes=[t_rs[ob]])
                P.op("dve", lambda e: e.reciprocal(out=rs_t[:, ob, 0:NH], in_=rs_t[:, ob, 0:NH]), reads=[t_rs[ob]], writes=[t_rs[ob]])

            def norm_mult(qb, g):
                ob = qb % 2
                h0, h1 = GRP[g]
                pbk = obank(qb, g)
                nh = h1 - h0
                pv3 = ps[pbk][:, 0:nh * 65].rearrange("p (h d) -> p h d", d=65)
                P.op("dve", lambda e: e.tensor_tensor(
                    out=otok[ob][:, h0 * 64:h1 * 64].rearrange("p (h d) -> p h d", d=64), in0=pv3[:, :, 0:64],
                    in1=rs_t[:, ob, h0:h1].unsqueeze(2).broadcast_to([128, nh, 64]), op=ALU.mult),
                    reads=[t_ps[pbk], t_rs[ob]], writes=[t_otok3[ob][g]])

            def transposes(qb, copy_half=None):
                ob = qb % 2
                pb = 6
                pbv = ps[pb][:].bitcast(BF16)
                q0 = qb * 128
                if copy_half in (None, 0):
                    for c in range(NCH):
                        P.op("pe", lambda e, c=c: e.transpose(out=pbv[:, c * 128:(c + 1) * 128],
                                                              in_=otok[ob][:, c * 128:(c + 1) * 128], identity=ident[:]),
                             reads=t_otok3[ob] + [t_ident], writes=[t_ps[pb]], pe_acc=(c > 0))
                halves = [0, 1] if copy_half is None else [copy_half]
                for hf in halves:
                    c0 = hf * 4
                    P.op("act", lambda e, c0=c0: e.activation(out=OT_v[:, c0:c0 + 4, q0:q0 + 128],
                                                             in_=pbv[:, c0 * 128:(c0 + 4) * 128].rearrange("p (c t) -> p c t", c=4), func=AF.Copy),
                         reads=[t_ps[pb]], writes=[t for c in range(c0, c0 + 4) for t in t_OT(c, q0, 128)])

            seq = [(qb, oc) for qb in range(nqb) for oc in range(NCH)]
            LA = 3
            pts = {i: qk(*seq[i]) for i in range(LA)}
            for idx, (qb, oc) in enumerate(seq):
                if idx + LA < len(seq):
                    pts[idx + LA] = qk(*seq[idx + LA])
                pv(qb, oc, pts.pop(idx))
                if qb >= 1:
                    if oc == 0:
                        norm_sums(qb - 1)
                    elif oc == 1:
                        norm_mult(qb - 1, 1)
                    elif oc == 2:
                        norm_mult(qb - 1, 2)
                    elif oc == 3:
                        norm_mult(qb - 1, 0)
                    elif oc == 4:
                        transposes(qb - 1, 0)
                    elif oc == 5:
                        transposes(qb - 1, 1)
            norm_sums(nqb - 1)
            for g in (1, 2, 0):
                norm_mult(nqb - 1, g)
            transposes(nqb - 1)
            proj_resid(kview(w_o[0]), lambda kc, t0, n: OT_v[:, kc, t0 - base:t0 - base + n],
                       lambda kc, t0, n: t_OT(kc, t0 - base, n), NCH, blocks, (lambda oc: dcol("cbo", oc)),
                       ln_cb=lambda t0, n, last: ln_stream(t0, n, "lmg1", "lmb1", "albm1", defer=last))

        plan(0)
        plan(1)
        for sbi in range(2):
            if sbi == 0:
                T = T0
                l0_blocks = [(0, 384), (384, 384), (768, 384)]
                l1_blocks = [(128, 512), (640, 512)]
                xs0, xn = 0, XC
                xdst0 = 0
                out_off = -128
            else:
                T = T1
                l0_blocks = [(0, 512), (512, 512)]
                l1_blocks = [(0, 512), (512, 512)]
                xs0, xn = XC, T1
                xdst0 = CH
                out_off = 1024
            if sbi == 0:
                pieces = [(0, CH + 384), (CH + 384, 384), (CH + 768, 384)]
                xT3 = xT.rearrange("(c p) t -> p c t", p=128)
                for pi, (c0, ncol) in enumerate(pieces):
                    P.op("pool", lambda e, c0=c0, ncol=ncol: e.dma_start(out=xb[:, :, c0:c0 + ncol], in_=xT3[:, :, c0:c0 + ncol]),
                         writes=[t for c in range(NCH) for t in tlx(c, c0, ncol)], dma_sem=f"xbp_{c0}")
                    if pi == 0:
                        wprefetch(2)
                P.op("pool", lambda e: e.dma_start(out=ident[:], in_=ident_d), writes=[t_ident], dma_sem="c3")
            def load_S(extra=None, T=T, xs0=xs0, xn=xn):
                for c in range(NCH):
                    P.op("sp", lambda e, c=c: e.dma_start(out=S[:, c, 0:T], in_=xT[c * 128:(c + 1) * 128, xs0 + xn - T:xs0 + xn]),
                         writes=tl(tS, c, 0, T), dma_sem=f"x{c}", extra=extra)
            flush()
            if debug_stage == "load0":
                break
            try:
                chk("load")
                conv_module(sbi, l0_blocks, T, load_S)
            except _Stop:
                break
            if debug_stage == "l0mix":
                break
            ffn(0, l0_blocks, "lfg0", "lfb0", "albf0", False, 0)
            if debug_stage == "l0":
                break
            q_reset()

            def q_early(l1_blocks=l1_blocks):
                q_prep()
                for oc in range(6):
                    q_unit(oc, 0, l1_blocks, l1_blocks[0][0], ahead=min(2, 5 - oc))
                    drip(2)
            kv_proj(sbi, l0_blocks, T, before_last=q_early)
            if debug_stage == "kv":
                break
            try:
                attention(sbi, l1_blocks)
            except _Stop:
                break
            if debug_stage == "l1mix":
                break
            def sb1_prefetch(t0, n):
                bi = (t0 - 128) // 512
                for c in range(NCH):
                    P.op("pool", lambda e, c=c, bi=bi: e.dma_start(out=xb[:, c, CH + bi * 512:CH + (bi + 1) * 512],
                                                                  in_=xT[c * 128:(c + 1) * 128, XC + bi * 512:XC + (bi + 1) * 512]),
                         writes=tlx(c, CH + bi * 512, 512), dma_sem=f"xn{c}_{bi}")
            ffn(1, l1_blocks, "lfg1", "lfb1", None, True, out_off, after_stats=(sb1_prefetch if sbi == 0 else None))

        flush()
        if debug_stage is not None:
            dbg = nc.dram_tensor("dbgS", [D, T0], F32, kind="ExternalOutput").ap()
            for c in range(NCH):
                P.op("sp", lambda e, c=c: e.dma_start(out=dbg[c * 128:(c + 1) * 128, :], in_=S[:, c, :]),
                     reads=tl(tS, c, 0, T0), dma_sem="dbg")
        P.emit()
        nc._prog_stats = {e: len(P.ops[e]) for e in P.ENGS}
        nc._pe_tags = [o.tag for o in P.ops["pe"]]
    return nc


def make_in_maps(inputs):
    inp = {k: np.asarray(v) for k, v in inputs.items()}
    x = inp["x"][0]
    xt_full = np.ascontiguousarray(x.T)
    pad = CH + HALO
    xt_pad = np.concatenate([np.zeros((D, pad), np.float32), xt_full], axis=1)
    ident = np.eye(128, dtype=np.float32)
    afac = alibi_factor()
    bv_bc = np.ascontiguousarray(np.broadcast_to(inp["kv_b_v"][None, :], (128, 128))).astype(np.float32)
    sinks_bc = np.ascontiguousarray(np.broadcast_to(inp["attn_sinks"][0][None, :], (128, NH))).astype(np.float32)
    shared = {
        "bv_bc": bv_bc, "sinks_bc": sinks_bc, "ident": ident, "afac": afac,
        "conv_w_pw1": inp["conv_w_pw1"], "conv_w_pw2": inp["conv_w_pw2"],
        "kv_w_k": inp["kv_w_k"], "kv_w_v": inp["kv_w_v"],
        "attn_w_q": inp["attn_w_q"], "attn_w_o": inp["attn_w_o"],
        "ffn_w_gate": inp["ffn_w_gate"], "ffn_w_up": inp["ffn_w_up"], "ffn_w_down": inp["ffn_w_down"],
    }
    maps = []
    for r in range(NCORES):
        m = dict(shared)
        m["xT"] = np.ascontiguousarray(xt_pad[:, r * TPC:r * TPC + XTOT])
        m["params"] = pack_params(inp, r)
        maps.append(m)
    return maps


_NC_CACHE = {}


def kernel(**inputs):
    if "nc" not in _NC_CACHE:
        _NC_CACHE["nc"] = build_nc()
    nc = _NC_CACHE["nc"]
    maps = make_in_maps(inputs)
    res = run_bass_kernel_spmd(nc, maps, core_ids=list(range(NCORES)))
    out = np.empty((1, SEQ, D), np.float32)
    for r in range(NCORES):
        out[0, r * TPC:(r + 1) * TPC, :] = res.results[r]["outT"].T
    return out
```
